# Optimizing a Trainium2 kernel written in Bass

```python
import math
import jax
import jax.numpy as jnp
from jax import lax
import numpy as np

D_MODEL = 2048
BATCH = 4
SEQ = 4096
DEPTH = 2

CTX_LEN = 256
GRID_W = 64
Q_BLOCK = 128
ROPE_THETA = 10000.0
EPS = 1e-6

MLA_HEADS = 8
MLA_Q_LORA = 512
MLA_KV_LORA = 256
MLA_NOPE_DIM = 128
MLA_ROPE_DIM = 64
MLA_V_DIM = 128
MLA_WIDTH = MLA_HEADS * MLA_V_DIM

GQA_HEADS = 8
GQA_KV_HEADS = 2
GQA_GROUP = GQA_HEADS // GQA_KV_HEADS
GQA_HEAD_DIM = 128
GQA_WIDTH = GQA_HEADS * GQA_HEAD_DIM
GQA_KV_WIDTH = GQA_KV_HEADS * GQA_HEAD_DIM

SSD_INNER = D_MODEL
SSD_HEAD_DIM = 64
SSD_HEADS = SSD_INNER // SSD_HEAD_DIM
SSD_GROUPS = 4
SSD_HEADS_PER_GROUP = SSD_HEADS // SSD_GROUPS
SSD_STATE = 128
SSD_CONV = 5
SSD_CHUNK = 128
SSD_CONV_DIM = SSD_INNER + 2 * SSD_GROUPS * SSD_STATE

N_BRANCH = 3
IN_SPLITS = (MLA_Q_LORA, MLA_KV_LORA, MLA_ROPE_DIM, MLA_WIDTH,
             GQA_WIDTH, GQA_KV_WIDTH, GQA_KV_WIDTH, GQA_WIDTH,
             SSD_INNER, SSD_CONV_DIM, 2 * SSD_HEADS,
             N_BRANCH * D_MODEL)
IN_WIDTH = sum(IN_SPLITS)
IN_OFFSETS = tuple(int(v) for v in np.cumsum(IN_SPLITS)[:-1])

DEEPNORM_ALPHA = (2 * DEPTH) ** 0.25
DEEPNORM_BETA = (8 * DEPTH) ** -0.25

kernel_name = 'hybrid_mla_gqa_ssd_prefix_trunk'


def layer_norm(x):
    xf = x.astype(jnp.float32)
    mu = jnp.mean(xf, axis=-1, keepdims=True)
    var = jnp.mean(jnp.square(xf - mu), axis=-1, keepdims=True)
    return ((xf - mu) * lax.rsqrt(var + EPS)).astype(x.dtype)


def layer_norm_affine(x, g, b):
    return layer_norm(x) * g + b


def rms_norm(x, g):
    xf = x.astype(jnp.float32)
    y = xf * lax.rsqrt(jnp.mean(jnp.square(xf), axis=-1, keepdims=True) + EPS)
    return (y * g).astype(x.dtype)


def modulate(x, shift, scale):
    return layer_norm(x) * (1.0 + scale) + shift


def split_columns(p):
    return jnp.split(p, IN_OFFSETS, axis=-1)


def axial_rope_tables(rows, dim):
    row, col = jnp.meshgrid(jnp.arange(rows, dtype=jnp.float32),
                            jnp.arange(GRID_W, dtype=jnp.float32), indexing='ij')
    half = dim // 2
    inv_freq = ROPE_THETA ** (-jnp.arange(0, half, 2, dtype=jnp.float32) / half)
    ang_r = row.reshape(-1, 1) * inv_freq
    ang_c = col.reshape(-1, 1) * inv_freq
    ang = jnp.concatenate([ang_r, ang_r, ang_c, ang_c], axis=-1)
    return jnp.cos(ang), jnp.sin(ang)


def apply_rope(x, cos, sin):
    x1, x2, x3, x4 = jnp.split(x, 4, axis=-1)
    rot = jnp.concatenate([-x2, x1, -x4, x3], axis=-1)
    shape = (1, x.shape[1]) + (1,) * (x.ndim - 3) + (x.shape[-1],)
    return (x * cos.reshape(shape) + rot * sin.reshape(shape)).astype(x.dtype)


def mla_q(cq, q_norm, w_uq, cos, sin):
    b, n, _ = cq.shape
    q = (rms_norm(cq, q_norm) @ w_uq).reshape(b, n, MLA_HEADS, MLA_NOPE_DIM + MLA_ROPE_DIM)
    q_nope, q_pe = q[..., :MLA_NOPE_DIM], q[..., MLA_NOPE_DIM:]
    if cos is not None:
        q_pe = apply_rope(q_pe, cos, sin)
    return jnp.concatenate([q_nope, q_pe], axis=-1)[:, :, :, None, :]


def mla_kv(ckv, kr, kv_norm, w_ukv, cos, sin):
    b, n, _ = ckv.shape
    kv = (rms_norm(ckv, kv_norm) @ w_ukv).reshape(b, n, MLA_HEADS, MLA_NOPE_DIM + MLA_V_DIM)
    k_nope, v = kv[..., :MLA_NOPE_DIM], kv[..., MLA_NOPE_DIM:]
    k_pe = kr[:, :, None, :]
    if cos is not None:
        k_pe = apply_rope(k_pe, cos, sin)
    k = jnp.concatenate([k_nope, jnp.broadcast_to(k_pe, (b, n, MLA_HEADS, MLA_ROPE_DIM))], axis=-1)
    return k, v


def gqa_q(gq, q_norm, cos, sin):
    b, n, _ = gq.shape
    q = rms_norm(gq.reshape(b, n, GQA_KV_HEADS, GQA_GROUP, GQA_HEAD_DIM), q_norm)
    if cos is not None:
        q = apply_rope(q, cos, sin)
    return q


def gqa_kv(gk, gv, k_norm, cos, sin):
    b, n, _ = gk.shape
    k = rms_norm(gk.reshape(b, n, GQA_KV_HEADS, GQA_HEAD_DIM), k_norm)
    if cos is not None:
        k = apply_rope(k, cos, sin)
    return k, gv.reshape(b, n, GQA_KV_HEADS, GQA_HEAD_DIM)


def attend_latent(q, k_lat, v_lat, k_ctx, v_ctx):
    b, n, kvh, grp, dk = q.shape
    scale = dk ** -0.5
    n_keys = k_lat.shape[1]
    q_blocks = jnp.moveaxis(q.reshape(b, n // Q_BLOCK, Q_BLOCK, kvh, grp, dk), 1, 0)

    def one_block(qb):
        logits = jnp.concatenate([jnp.einsum('bqhgd,bkhd->bhgqk', qb, k_lat),
                                  jnp.einsum('bqhgd,bkhd->bhgqk', qb, k_ctx)], axis=-1)
        p = jax.nn.softmax(logits.astype(jnp.float32) * scale, axis=-1).astype(v_lat.dtype)
        return (jnp.einsum('bhgqk,bkhd->bqhgd', p[..., :n_keys], v_lat)
                + jnp.einsum('bhgqk,bkhd->bqhgd', p[..., n_keys:], v_ctx))

    out = lax.map(one_block, q_blocks)
    return jnp.moveaxis(out, 0, 1).reshape(b, n, -1)


def attend_context(q, k_ctx, v_ctx):
    b, n = q.shape[:2]
    logits = jnp.einsum('bqhgd,bkhd->bhgqk', q, k_ctx).astype(jnp.float32) * q.shape[-1] ** -0.5
    p = jax.nn.softmax(logits, axis=-1).astype(v_ctx.dtype)
    return jnp.einsum('bhgqk,bkhd->bqhgd', p, v_ctx).reshape(b, n, -1)


def centred_depthwise_conv(x, w, bias):
    pad = SSD_CONV // 2
    y = lax.conv_general_dilated(x, w[:, None, :].astype(x.dtype), window_strides=(1,),
                                 padding=((pad, pad),), dimension_numbers=('NWC', 'WIO', 'NWC'),
                                 feature_group_count=x.shape[-1])
    return jax.nn.silu(y + bias)


def ssd_scan(xs, dt, a, bm, cm, h0, want_y):
    b, n, g, e, p = xs.shape
    ns = bm.shape[-1]
    nc = n // SSD_CHUNK
    xs = xs.reshape(b, nc, SSD_CHUNK, g, e, p)
    dt = dt.reshape(b, nc, SSD_CHUNK, g, e)
    bm = bm.reshape(b, nc, SSD_CHUNK, g, ns)
    cm = cm.reshape(b, nc, SSD_CHUNK, g, ns)
    xdt = xs * dt[..., None]
    a_cum = jnp.cumsum(dt * a, axis=2)
    decay_to_end = jnp.exp(a_cum[:, :, -1:] - a_cum)
    chunk_states = jnp.einsum('bcjgn,bcjgep->bcgepn', bm, xdt * decay_to_end[..., None])
    chunk_decay = jnp.exp(a_cum[:, :, -1])
    seq_in = (jnp.moveaxis(chunk_decay, 1, 0), jnp.moveaxis(chunk_states, 1, 0))
    if not want_y:
        h_final, _ = lax.scan(lambda h, s: (s[0][..., None, None] * h + s[1], None), h0, seq_in)
        return None, h_final

    def carry_step(h, s):
        dec, st, c_blk, a_blk = s
        y_off = jnp.einsum('bign,bgepn->bigep', c_blk, h) * jnp.exp(a_blk)[..., None]
        return dec[..., None, None] * h + st, y_off

    h_final, y_off = lax.scan(carry_step, h0,
                              seq_in + (jnp.moveaxis(cm, 1, 0), jnp.moveaxis(a_cum, 1, 0)))
    y_off = jnp.moveaxis(y_off, 0, 1)
    lower_tri = jnp.tril(jnp.ones((SSD_CHUNK, SSD_CHUNK), dtype=bool))
    seg = a_cum[:, :, :, None] - a_cum[:, :, None, :]
    decay = jnp.exp(jnp.where(lower_tri[:, :, None, None], seg, -jnp.inf))
    cb = jnp.einsum('bcign,bcjgn->bcijg', cm, bm)
    y_diag = jnp.einsum('bcijge,bcjgep->bcigep', cb[..., None] * decay, xdt)
    return (y_diag + y_off).reshape(b, n, g * e * p), h_final


def ssd_mixer(xbc_lat, dt_lat, xbc_ctx, dt_ctx, conv_w, conv_b, a_log, dt_bias, d_skip, with_ctx_out):
    f32 = jnp.float32
    G, E, P, N = SSD_GROUPS, SSD_HEADS_PER_GROUP, SSD_HEAD_DIM, SSD_STATE
    a = -jnp.exp(a_log.astype(f32)).reshape(2, G, E)
    d = d_skip.astype(f32).reshape(G, E, 1)

    def prep(xbc, dt_raw):
        b, n, _ = xbc.shape
        xbc = centred_depthwise_conv(xbc, conv_w, conv_b)
        xs, bm, cm = jnp.split(xbc, (SSD_INNER, SSD_INNER + G * N), axis=-1)
        dt = jax.nn.softplus(dt_raw.astype(f32).reshape(b, n, 2, G, E)
                             + dt_bias.astype(f32).reshape(2, G, E))
        return (xs.astype(f32).reshape(b, n, G, E, P), bm.astype(f32).reshape(b, n, G, N),
                cm.astype(f32).reshape(b, n, G, N), dt)

    def flip(t):
        return jnp.flip(t, axis=1)

    def bidirectional(xs, bm, cm, dt, h_fwd, h_bwd, want_y):
        y_f, hf = ssd_scan(xs, dt[:, :, 0], a[0], bm, cm, h_fwd, want_y)
        y_b, hb = ssd_scan(flip(xs), flip(dt[:, :, 1]), a[1], flip(bm), flip(cm), h_bwd, want_y)
        y = None
        if want_y:
            b, n = xs.shape[:2]
            y = y_f + flip(y_b) + (d * xs).reshape(b, n, SSD_INNER)
        return y, hf, hb

    xc, bc, cc, dtc = prep(xbc_ctx, dt_ctx)
    h0 = jnp.zeros((xc.shape[0], G, E, P, N), f32)
    y_ctx, h_f, h_b = bidirectional(xc, bc, cc, dtc, h0, h0, with_ctx_out)
    xl, bl, cl, dtl = prep(xbc_lat, dt_lat)
    y_lat, _, _ = bidirectional(xl, bl, cl, dtl, h_f, h_b, True)
    if with_ctx_out:
        y_ctx = y_ctx.astype(xbc_ctx.dtype)
    return y_lat.astype(xbc_lat.dtype), y_ctx


def gated_group_rms_norm(y, z, g):
    b, n, _ = y.shape
    v = (y * jax.nn.silu(z)).astype(jnp.float32).reshape(b, n, SSD_GROUPS, -1)
    v = v * lax.rsqrt(jnp.mean(jnp.square(v), axis=-1, keepdims=True) + EPS)
    return (v.reshape(b, n, -1) * g).astype(y.dtype)


def merge_branches(ya, ga, yb, gb, yc, zc, merge_logits, ssd_norm, w_br_a, w_br_b, w_br_c, w_out):
    branch_a = (ya * jax.nn.silu(ga)) @ w_br_a
    branch_b = (yb * jax.nn.silu(gb)) @ w_br_b
    branch_c = gated_group_rms_norm(yc, zc, ssd_norm) @ w_br_c
    g_a, g_b, g_c = jnp.split(jax.nn.sigmoid(merge_logits), N_BRANCH, axis=-1)
    return (g_a * branch_a + g_b * branch_b + g_c * branch_c) @ w_out


def setup_inputs(seed: int = 0) -> dict:
    key = jax.random.key(seed)
    ks = jax.random.split(key, 26)
    f32 = jnp.float32
    L, D = DEPTH, D_MODEL

    def dense(k, shape, fan_in, mult=1.0):
        return jax.random.normal(k, shape, f32) * (mult * fan_in ** -0.5)

    def gain(k, shape):
        return 1.0 + 0.02 * jax.random.normal(k, shape, f32)

    def small(k, shape):
        return 0.02 * jax.random.normal(k, shape, f32)

    dt0 = jnp.exp(jax.random.uniform(ks[16], (L, 2, SSD_HEADS), f32, math.log(1e-3), math.log(1e-1)))
    return {
        'x': jax.random.normal(ks[0], (BATCH, SEQ, D), f32),
        'c': jax.random.normal(ks[1], (BATCH, D), f32),
        'ctx': jax.random.normal(ks[2], (BATCH, CTX_LEN, D), f32),
        'c_ctx': jax.random.normal(ks[3], (D,), f32),
        'w_mod': dense(ks[4], (L, D, 3 * D), D, 0.5),
        'b_mod': small(ks[5], (L, 3 * D)),
        'w_in': dense(ks[6], (L, D, IN_WIDTH), D),
        'mla_q_norm': gain(ks[7], (L, MLA_Q_LORA)),
        'mla_w_uq': dense(ks[8], (L, MLA_Q_LORA, MLA_HEADS * (MLA_NOPE_DIM + MLA_ROPE_DIM)), MLA_Q_LORA),
        'mla_kv_norm': gain(ks[9], (L, MLA_KV_LORA)),
        'mla_w_ukv': dense(ks[10], (L, MLA_KV_LORA, MLA_HEADS * (MLA_NOPE_DIM + MLA_V_DIM)), MLA_KV_LORA),
        'gqa_q_norm': gain(ks[11], (L, GQA_HEAD_DIM)),
        'gqa_k_norm': gain(ks[12], (L, GQA_HEAD_DIM)),
        'ssd_conv_w': dense(ks[13], (L, SSD_CONV, SSD_CONV_DIM), SSD_CONV),
        'ssd_conv_b': small(ks[14], (L, SSD_CONV_DIM)),
        'ssd_a_log': jnp.log(jax.random.uniform(ks[15], (L, 2, SSD_HEADS), f32, 1.0, 16.0)),
        'ssd_dt_bias': dt0 + jnp.log(-jnp.expm1(-dt0)),
        'ssd_d': gain(ks[17], (L, SSD_HEADS)),
        'ssd_norm': gain(ks[18], (L, SSD_INNER)),
        'w_br_a': dense(ks[19], (L, MLA_WIDTH, D), MLA_WIDTH, DEEPNORM_BETA),
        'w_br_b': dense(ks[20], (L, GQA_WIDTH, D), GQA_WIDTH, DEEPNORM_BETA),
        'w_br_c': dense(ks[21], (L, SSD_INNER, D), SSD_INNER, DEEPNORM_BETA),
        'w_out': dense(ks[22], (L, D, D), D, DEEPNORM_BETA),
        'ln_g': gain(ks[23], (L, D)),
        'ln_b': small(ks[24], (L, D)),
    }


def reference(x, c, ctx, c_ctx, w_mod, b_mod, w_in, mla_q_norm, mla_w_uq, mla_kv_norm, mla_w_ukv,
              gqa_q_norm, gqa_k_norm, ssd_conv_w, ssd_conv_b, ssd_a_log, ssd_dt_bias, ssd_d, ssd_norm,
              w_br_a, w_br_b, w_br_c, w_out, ln_g, ln_b):
    ROWS = x.shape[1] // GRID_W
    cos_a, sin_a = axial_rope_tables(ROWS, MLA_ROPE_DIM)
    cos_b, sin_b = axial_rope_tables(ROWS, GQA_HEAD_DIM)
    h_ctx = ctx
    for l in range(DEPTH):
        ctx_out = l < DEPTH - 1
        mod = jax.nn.silu(c) @ w_mod[l] + b_mod[l]
        shift, scale, gate = jnp.split(mod[:, None, :], 3, axis=-1)
        shift_c, scale_c, gate_c = jnp.split(jax.nn.silu(c_ctx) @ w_mod[l] + b_mod[l], 3, axis=-1)
        (cq, ckv, kr, ga, gq, gk, gv, gb, z, xbc, dtr, mg) = split_columns(modulate(x, shift, scale) @ w_in[l])
        (cq_c, ckv_c, kr_c, ga_c, gq_c, gk_c, gv_c, gb_c, z_c, xbc_c, dtr_c, mg_c) = split_columns(
            modulate(h_ctx, shift_c, scale_c) @ w_in[l])

        ka, va = mla_kv(ckv, kr, mla_kv_norm[l], mla_w_ukv[l], cos_a, sin_a)
        ka_c, va_c = mla_kv(ckv_c, kr_c, mla_kv_norm[l], mla_w_ukv[l], None, None)
        ya = attend_latent(mla_q(cq, mla_q_norm[l], mla_w_uq[l], cos_a, sin_a), ka, va, ka_c, va_c)
        kb, vb = gqa_kv(gk, gv, gqa_k_norm[l], cos_b, sin_b)
        kb_c, vb_c = gqa_kv(gk_c, gv_c, gqa_k_norm[l], None, None)
        yb = attend_latent(gqa_q(gq, gqa_q_norm[l], cos_b, sin_b), kb, vb, kb_c, vb_c)
        yc, yc_c = ssd_mixer(xbc, dtr, xbc_c, dtr_c, ssd_conv_w[l], ssd_conv_b[l], ssd_a_log[l],
                             ssd_dt_bias[l], ssd_d[l], ctx_out)
        out = merge_branches(ya, ga, yb, gb, yc, z, mg, ssd_norm[l], w_br_a[l], w_br_b[l], w_br_c[l], w_out[l])

        if ctx_out:
            ya_c = attend_context(mla_q(cq_c, mla_q_norm[l], mla_w_uq[l], None, None), ka_c, va_c)
            yb_c = attend_context(gqa_q(gq_c, gqa_q_norm[l], None, None), kb_c, vb_c)
            out_c = merge_branches(ya_c, ga_c, yb_c, gb_c, yc_c, z_c, mg_c, ssd_norm[l],
                                   w_br_a[l], w_br_b[l], w_br_c[l], w_out[l])
            h_ctx = layer_norm_affine(DEEPNORM_ALPHA * h_ctx + gate_c * out_c, ln_g[l], ln_b[l])
        x = layer_norm_affine(DEEPNORM_ALPHA * x + gate * out, ln_g[l], ln_b[l])
    return x
```

```python
import math
from contextlib import ExitStack

import numpy as np
import concourse.bass as bass
import concourse.mybir as mybir
from concourse.bass_utils import run_bass_kernel_spmd

F32 = mybir.dt.float32
BF16 = mybir.dt.bfloat16
U8 = mybir.dt.uint8
AF = mybir.ActivationFunctionType
ALU = mybir.AluOpType

D = 2048
DEPTH = 2
GRID_W = 64
EPS = 1e-6
SPL = (512, 256, 64, 1024, 1024, 256, 256, 1024, 2048, 3072, 64, 6144)
OFF = [0]
for _s in SPL:
    OFF.append(OFF[-1] + _s)
(O_CQ, O_CKV, O_KR, O_GA, O_GQ, O_GK, O_GV, O_GB, O_Z, O_XBC, O_DT, O_MG) = OFF[:12]
INW = OFF[12]
ALPHA = (2 * DEPTH) ** 0.25
ENGS = ("pe", "act", "dve", "pool", "sp")
import os as _os
DSTOP = int(_os.environ.get("DSTOP", "0"))
NSEM_ROT = 6


class _Op:
    __slots__ = ("eng", "fn", "deps", "dma", "semkey", "idx", "sig", "sigval", "cidx", "epoch")

    def __init__(self, eng, fn, dma, semkey):
        self.eng = eng
        self.fn = fn
        self.deps = []
        self.dma = dma
        self.semkey = semkey
        self.sig = False
        self.sigval = None


class Sched:
    def __init__(self, nc):
        self.nc = nc
        self.q = {e: [] for e in ENGS}
        self.lastw = {}
        self.readers = {}
        self.ccount = {}
        self.lastdma = {}
        self.all = []
        self.epoch = 0

    def op(self, eng, fn, reads=(), writes=(), dma=False, semkey=None, extra=()):
        o = _Op(eng, fn, dma, semkey)
        o.idx = len(self.q[eng])
        o.epoch = self.epoch
        o.cidx = self.ccount.get(eng, 0)
        if not dma:
            self.ccount[eng] = o.cidx + 1
        deps = list(extra)
        for r in reads:
            w = self.lastw.get(r)
            if w is not None:
                deps.append(w)
        for w_ in writes:
            w = self.lastw.get(w_)
            if w is not None:
                deps.append(w)
            deps.extend(self.readers.get(w_, ()))
        seen = set()
        for d in deps:
            if d is o or id(d) in seen:
                continue
            seen.add(id(d))
            if (not d.dma) and d.eng == eng and not dma:
                if eng == "pe" or d.cidx < o.cidx - 1:
                    continue
            o.deps.append(d)
            d.sig = True
        for r in reads:
            lst = self.readers.setdefault(r, [])
            if not dma:
                for i_, x_ in enumerate(lst):
                    if (not x_.dma) and x_.eng == eng:
                        lst[i_] = o
                        break
                else:
                    lst.append(o)
            else:
                lst.append(o)
        for w_ in writes:
            self.lastw[w_] = o
            self.readers[w_] = []
        self.q[eng].append(o)
        self.all.append(o)
        if dma:
            o.semkey = (self.epoch, semkey)
            self.lastdma[semkey] = o
        return o

    def barrier(self):
        last = []
        for e in ENGS:
            for o in reversed(self.q[e]):
                if not o.dma:
                    last.append(o)
                    break
        last.extend(self.lastdma.values())
        self.lastdma = {}
        for e in ENGS:
            self.op(e, lambda eng: eng.nop() if hasattr(eng, "nop") else eng.engine_nop(), extra=last)
        self.lastw = {}
        self.readers = {}
        self.epoch += 1

    def emit(self, esem, dsem):
        keymap = {}
        semcnt = [0] * len(dsem)
        nper = {}
        cnt = {}
        for o in self.all:
            if o.dma:
                k = o.semkey
                if k not in keymap:
                    i = nper.get(k[0], 0)
                    nper[k[0]] = i + 1
                    if i >= len(dsem):
                        raise RuntimeError("out of dma semaphores")
                    keymap[k] = i
                i = keymap[k]
                semcnt[i] += 16
                o.sigval = (dsem[i], semcnt[i])
            elif o.sig:
                kk_ = (o.eng, o.epoch % NSEM_ROT)
                cnt[kk_] = cnt.get(kk_, 0) + 1
                o.sigval = (esem[kk_], cnt[kk_])

        def run(eng_name, eng):
            waited = {}
            for o in self.q[eng_name]:
                need = {}
                for d in o.deps:
                    s, v = d.sigval
                    key = id(s)
                    if waited.get(key, 0) >= v:
                        continue
                    if key not in need or need[key][1] < v:
                        need[key] = (s, v)
                for key, (s, v) in need.items():
                    eng.wait_ge(s, v)
                    waited[key] = v
                ins = o.fn(eng)
                if o.dma:
                    ins.then_inc(o.sigval[0], 16)
                elif o.sig:
                    ins.then_inc(o.sigval[0], 1)

        return run


class Arena:
    def __init__(self, ap_u8, size):
        self.ap = ap_u8
        self.size = size
        self.off = 0

    def reset(self):
        self.off = 0

    def alloc(self, free_shape, dt):
        esz = 4 if dt == F32 else 2
        n = 1
        for s in free_shape:
            n *= s
        nb = (n * esz + 63) // 64 * 64
        if self.off + nb > self.size:
            raise RuntimeError("arena overflow: %d + %d > %d" % (self.off, nb, self.size))
        v = self.ap[:, self.off:self.off + n * esz].bitcast(dt)
        self.off += nb
        if len(free_shape) == 2:
            v = v.rearrange("p (a b) -> p a b", a=free_shape[0])
        elif len(free_shape) == 3:
            v = v.rearrange("p (a b c) -> p a b c", a=free_shape[0], b=free_shape[1])
        return v


def _rope_tables(seq, ctx, dim):
    rows = seq // GRID_W
    r, c = np.meshgrid(np.arange(rows, dtype=np.float32), np.arange(GRID_W, dtype=np.float32), indexing="ij")
    half = dim // 2
    inv = (10000.0 ** (-np.arange(0, half, 2, dtype=np.float32) / half)).astype(np.float32)
    ar = r.reshape(-1, 1) * inv
    ac = c.reshape(-1, 1) * inv
    ang = np.concatenate([ar, ar, ac, ac], axis=-1).astype(np.float32)
    cos = np.concatenate([np.cos(ang), np.ones((ctx, dim), np.float32)], 0)
    sin = np.concatenate([np.sin(ang), np.zeros((ctx, dim), np.float32)], 0)
    return np.ascontiguousarray(cos.T.astype(np.float32)), np.ascontiguousarray(sin.T.astype(np.float32))


def _rot_matrix(dim):
    q = dim // 4
    R = np.zeros((dim, dim), np.float32)
    for dp in range(dim):
        blk = dp // q
        if blk in (0, 2):
            R[dp + q, dp] = -1.0
        else:
            R[dp - q, dp] = 1.0
    return R


def host_consts(seq, ctx):
    cosA, sinA = _rope_tables(seq, ctx, 64)
    cosB, sinB = _rope_tables(seq, ctx, 128)
    k = np.arange(128)
    U = (k[:, None] <= k[None, :]).astype(np.float32)
    L = (k[:, None] >= k[None, :]).astype(np.float32)
    nmf = np.where(k[:, None] <= k[None, :], 0.0, -30000.0).astype(np.float32)
    nmb = np.where(k[:, None] >= k[None, :], 0.0, -30000.0).astype(np.float32)
    return {
        "k_cosA": cosA, "k_sinA": sinA, "k_cosB": cosB, "k_sinB": sinB,
        "k_RA": _rot_matrix(64), "k_RB": _rot_matrix(128),
        "k_ident": np.eye(128, dtype=np.float32), "k_U": U, "k_L": L,
        "k_nmf": np.tile(nmf, (1, 4)), "k_nmb": np.tile(nmb, (1, 4)),
    }


def build_program(SEQ, CTX, depth=DEPTH, debug=False, stop=None):
    T = SEQ + CTX
    NT = T // 128
    NTL = SEQ // 128
    NTC = CTX // 128
    blocks = [(i * 512, 512) for i in range(SEQ // 512)] + [(SEQ, CTX)]
    nc = bass.Bass("TRN2", target_bir_lowering=False)

    def din(name, shape, dt=F32):
        return nc.dram_tensor(name, list(shape), dt, kind="ExternalInput").ap()

    def dscr(name, shape, dt):
        return nc.dram_tensor(name, list(shape), dt, kind=("ExternalOutput" if debug else "Internal")).ap()

    xin = din("xin", [T, D])
    c2 = din("c2", [2, D])
    W = {}
    for nm, shp in (("w_mod", [depth, D, 3 * D]), ("b_mod", [depth, 3 * D]), ("w_in", [depth, D, INW]),
                    ("mla_q_norm", [depth, 512]), ("mla_w_uq", [depth, 512, 1536]), ("mla_kv_norm", [depth, 256]),
                    ("mla_w_ukv", [depth, 256, 2048]), ("gqa_q_norm", [depth, 128]), ("gqa_k_norm", [depth, 128]),
                    ("ssd_conv_w", [depth, 5, 3072]), ("ssd_conv_b", [depth, 3072]), ("ssd_a_log", [depth, 64]),
                    ("ssd_dt_bias", [depth, 64]), ("ssd_d", [depth, 32]), ("ssd_norm", [depth, D]),
                    ("w_br", [depth, 4096, D]), ("w_out", [depth, D, D]), ("ln_g", [depth, D]), ("ln_b", [depth, D])):
        W[nm] = din(nm, shp)
    K = {}
    for nm, shp in (("k_cosA", [64, T]), ("k_sinA", [64, T]), ("k_cosB", [128, T]), ("k_sinB", [128, T]),
                    ("k_RA", [64, 64]), ("k_RB", [128, 128]), ("k_ident", [128, 128]), ("k_U", [128, 128]),
                    ("k_L", [128, 128]), ("k_nmf", [128, 512]), ("k_nmb", [128, 512])):
        K[nm] = din(nm, shp)
    out = nc.dram_tensor("out", [SEQ, D], F32, kind="ExternalOutput").ap()

    XR = dscr("XR", [T, D], F32)
    PJ = dscr("PJ", [INW, T], BF16)
    ZT = dscr("ZT", [T, 2048], BF16)
    GV = dscr("GV", [T, 256], BF16)
    DTR = dscr("DTR", [T, 64], F32)
    QA = dscr("QA", [8, 192, T], BF16)
    KAN = dscr("KAN", [8, 128, T], BF16)
    KPE = dscr("KPE", [64, T], BF16)
    VA = dscr("VA", [T, 1024], BF16)
    QB = dscr("QB", [8, 128, T], BF16)
    KB = dscr("KB", [2, 128, T], BF16)
    YG = dscr("YG", [4096, T], BF16)
    XC = dscr("XC", [T, 3072], BF16)
    XBT = dscr("XBT", [1024, T], BF16)
    YF = dscr("YF", [T, 2048], F32)

    es = ExitStack()
    S = Sched(nc)
    ARENA_BYTES = 206 * 1024
    arena_t = es.enter_context(nc.sbuf_tensor("arena", [128, ARENA_BYTES], U8))
    AR = Arena(arena_t, ARENA_BYTES - 8 * 1024)
    CAR = Arena(arena_t[:, ARENA_BYTES - 8 * 1024:ARENA_BYTES], 8 * 1024)
    PS = [es.enter_context(nc.psum_tensor("ps%d" % i, [128, 512], F32)) for i in range(8)]

    uid = [0]

    def key(p):
        uid[0] += 1
        return (p, uid[0])

    def dma(eng, out_, in_, reads=(), writes=(), semkey=None, slow=False):
        if slow:
            f = lambda e: e.dma_start(out=out_, in_=in_, allow_slow_non_contiguous=True)
        else:
            f = lambda e: e.dma_start(out=out_, in_=in_)
        return S.op(eng, f, reads=reads, writes=writes, dma=True, semkey=semkey)

    ident = CAR.alloc([128], F32)
    identb = CAR.alloc([128], BF16)
    onesb = CAR.alloc([128], BF16)
    onesf = CAR.alloc([128], F32)
    RAb = CAR.alloc([64], BF16)
    RBb = CAR.alloc([128], BF16)
    Uf = CAR.alloc([128], F32)
    Lf = CAR.alloc([128], F32)
    nmf = CAR.alloc([512], F32)
    nmb = CAR.alloc([512], F32)
    dma("sp", ident, K["k_ident"], writes=["ident"], semkey="c0")
    dma("sp", Uf, K["k_U"], writes=["Uf"], semkey="c1")
    dma("sp", Lf, K["k_L"], writes=["Lf"], semkey="c2")
    dma("sp", nmf, K["k_nmf"], writes=["nmf"], semkey="c3")
    dma("sp", nmb, K["k_nmb"], writes=["nmb"], semkey="c4")
    dma("pool", identb, K["k_ident"], writes=["identb"], semkey="c5")
    dma("pool", RAb[0:64, :], K["k_RA"], writes=["RAb"], semkey="c6")
    dma("pool", RBb, K["k_RB"], writes=["RBb"], semkey="c7")
    S.op("dve", lambda e: e.memset(onesb, 1.0), writes=["onesb"])
    S.op("dve", lambda e: e.memset(onesf, 1.0), writes=["onesf"])
    CONST_R = ["ident", "identb", "onesb", "onesf", "RAb", "RBb", "Uf", "Lf", "nmf", "nmb"]

    def const_barrier():
        pass

    for l in range(depth):
        last = (l == depth - 1)
        ctx_out = not last
        xsrc = xin if l == 0 else XR
        xdst = out if last else XR

        S.barrier()
        AR.reset()
        c2T = AR.alloc([16, 2], F32)
        scT = AR.alloc([16, 2], F32)
        bmT = AR.alloc([48], F32)
        shT = AR.alloc([16, 2], F32)
        s1T = AR.alloc([16, 2], F32)
        gbc = AR.alloc([2, 2048], F32)
        A_KEEP = AR.off
        bgate = AR.alloc([2048], F32)
        scbc = AR.alloc([2, 16, 128], F32)
        for r_ in range(2):
            dma("sp", c2T[:, :, r_], c2[r_].rearrange("(kc p) -> p kc", p=128), writes=["c2T"], semkey=("a0", r_), slow=True)
        dma("sp", bmT, W["b_mod"][l].rearrange("(t p) -> p t", p=128), writes=["bmT"], semkey="a1", slow=True)
        dma("sp", bgate, W["b_mod"][l:l + 1, 2 * D:3 * D].partition_broadcast(128), writes=["bgate"], semkey="a2")
        S.op("act", lambda e: e.activation(out=scT, in_=c2T, func=AF.Silu), reads=["c2T"], writes=["scT"])
        for r in range(2):
            for kc in range(16):
                S.op("dve", lambda e, r=r, kc=kc: e.tensor_copy(out=scbc[:, r, kc, :], in_=scT[:, kc, r:r + 1].to_broadcast([128, 128])),
                     reads=["scT"], writes=["scbc"])
        wmb = [AR.alloc([16, 512], F32) for _ in range(2)]
        for cb in range(12):
            wb = wmb[cb % 2]
            wk = ("wmb", cb % 2)
            dma("sp" if cb % 2 == 0 else "act", wb, W["w_mod"][l, :, cb * 512:(cb + 1) * 512].rearrange("(kc p) c -> p kc c", p=128),
                writes=[wk], semkey=wk)
            if cb < 8:
                pst = PS[cb % 2]
                for ct in range(4):
                    for kc in range(16):
                        S.op("pe", lambda e, wb=wb, ct=ct, kc=kc, pst=pst: e.matmul(pst[:, ct * 2:ct * 2 + 2], lhsT=wb[:, kc, ct * 128:(ct + 1) * 128], rhs=scT[:, kc, :], start=(kc == 0), stop=(kc == 15)),
                             reads=[wk, "scT"], writes=[("ps", cb % 2)])
                for ct in range(4):
                    gt = cb * 4 + ct
                    dst = shT if gt < 16 else s1T
                    kk = gt % 16
                    if gt < 16:
                        S.op("dve", lambda e, pst=pst, ct=ct, dst=dst, kk=kk, gt=gt: e.tensor_scalar_add(out=dst[:, kk, :], in0=pst[:, ct * 2:ct * 2 + 2], scalar1=bmT[:, gt:gt + 1]),
                             reads=[("ps", cb % 2), "bmT"], writes=["modT"])
                    else:
                        S.op("dve", lambda e, pst=pst, ct=ct, dst=dst, kk=kk, gt=gt: e.tensor_scalar(out=dst[:, kk, :], in0=pst[:, ct * 2:ct * 2 + 2], scalar1=bmT[:, gt:gt + 1], scalar2=1.0, op0=ALU.add, op1=ALU.add),
                             reads=[("ps", cb % 2), "bmT"], writes=["modT"])
            else:
                gcb = cb - 8
                for r in range(2):
                    pst = PS[2 + r]
                    for kc in range(16):
                        S.op("pe", lambda e, wb=wb, kc=kc, pst=pst, r=r: e.matmul(pst[:, :], lhsT=scbc[:, r, kc, :], rhs=wb[:, kc, :], start=(kc == 0), stop=(kc == 15)),
                             reads=[wk, "scbc"], writes=[("ps", 2 + r)])
                    S.op("dve", lambda e, pst=pst, r=r, gcb=gcb: e.tensor_tensor(out=gbc[:, r, gcb * 512:(gcb + 1) * 512], in0=pst[:, :], in1=bgate[:, gcb * 512:(gcb + 1) * 512], op=ALU.add),
                         reads=[("ps", 2 + r), "bgate"], writes=["gbc"])

        if stop == "A":
            break
        S.barrier()
        AR.off = A_KEEP
        xmodT = AR.alloc([16, T], BF16)
        B_KEEP = AR.off
        xt = [AR.alloc([2048], F32) for _ in range(2)]
        stats = [AR.alloc([4, 6], F32) for _ in range(2)]
        mv = [AR.alloc([2], F32) for _ in range(2)]
        rstd = [AR.alloc([1], F32) for _ in range(2)]
        for t in range(NT):
            b = t % 2
            r = 0 if t < NTL else 1
            xk = ("xt", b)
            dma("sp", xt[b], xsrc[t * 128:(t + 1) * 128, :], writes=[xk], semkey=xk)
            for c in range(4):
                S.op("dve", lambda e, b=b, c=c: e.bn_stats(out=stats[b][:, c, :], in_=xt[b][:, c * 512:(c + 1) * 512]), reads=[xk], writes=[("st", b)])
            S.op("dve", lambda e, b=b: e.bn_aggr(out=mv[b], in_=stats[b]), reads=[("st", b)], writes=[("mv", b)])
            S.op("dve", lambda e, b=b: e.tensor_scalar_add(out=rstd[b], in0=mv[b][:, 1:2], scalar1=EPS), reads=[("mv", b)], writes=[("rs", b)])
            S.op("act", lambda e, b=b: e.activation(out=rstd[b], in_=rstd[b], func=AF.Sqrt), reads=[("rs", b)], writes=[("rs", b)])
            S.op("dve", lambda e, b=b: e.reciprocal(out=rstd[b], in_=rstd[b]), reads=[("rs", b)], writes=[("rs", b)])
            S.op("dve", lambda e, b=b: e.tensor_scalar(out=xt[b], in0=xt[b], scalar1=mv[b][:, 0:1], scalar2=rstd[b][:, 0:1], op0=ALU.subtract, op1=ALU.mult),
                 reads=[xk, ("mv", b), ("rs", b)], writes=[xk])
            for g in range(4):
                pb = (t * 4 + g) % 4
                pst = PS[pb]
                for j in range(4):
                    kc = g * 4 + j
                    S.op("pe", lambda e, b=b, kc=kc, j=j, pst=pst: e.transpose(out=pst[:, j * 128:(j + 1) * 128], in_=xt[b][:, kc * 128:(kc + 1) * 128], identity=ident),
                         reads=[xk], writes=[("ps", pb)])
                for j in range(4):
                    kc = g * 4 + j
                    if j % 2 == 0:
                        S.op("act", lambda e, kc=kc, j=j, pst=pst, t=t, r=r: e.activation(out=xmodT[:, kc, t * 128:(t + 1) * 128], in_=pst[:, j * 128:(j + 1) * 128], func=AF.Identity, bias=shT[:, kc, r:r + 1], scale=s1T[:, kc, r:r + 1]),
                             reads=[("ps", pb)], writes=[("xm", t)])
                    else:
                        S.op("dve", lambda e, kc=kc, j=j, pst=pst, t=t, r=r: e.tensor_scalar(out=xmodT[:, kc, t * 128:(t + 1) * 128], in0=pst[:, j * 128:(j + 1) * 128], scalar1=s1T[:, kc, r:r + 1], scalar2=shT[:, kc, r:r + 1], op0=ALU.mult, op1=ALU.add),
                             reads=[("ps", pb)], writes=[("xm", t)])

        if stop == "B":
            break
        S.barrier()
        AR.off = B_KEEP
        wfm = [AR.alloc([16, 128], BF16) for _ in range(3)]
        ofm = [AR.alloc([512], BF16) for _ in range(4)]
        wtm = [AR.alloc([16, 256], BF16) for _ in range(2)]
        otm = [AR.alloc([256], BF16) for _ in range(2)]
        otf = [AR.alloc([64], F32) for _ in range(2)]
        win = W["w_in"][l]
        fm_tiles = []
        for (o0, n) in ((O_CQ, 512), (O_CKV, 256), (O_KR, 64), (O_GA, 1024), (O_GQ, 1024), (O_GK, 256), (O_GB, 1024), (O_XBC, 3072), (O_MG, 6144)):
            for c0 in range(o0, o0 + n, 128):
                fm_tiles.append((c0, min(128, o0 + n - c0)))
        ev = 0
        for i, (c0, ncol) in enumerate(fm_tiles):
            wb = wfm[i % 3]
            wk = ("wfm", i % 3)
            dma("pool", wb[:, :, 0:ncol], win[:, c0:c0 + ncol].rearrange("(kc p) c -> p kc c", p=128), writes=[wk], semkey=wk)
            for bi, (t0, n) in enumerate(blocks):
                pb = ev % 4
                pst = PS[pb]
                for kc in range(16):
                    S.op("pe", lambda e, wb=wb, kc=kc, pst=pst, t0=t0, n=n, ncol=ncol: e.matmul(pst[0:ncol, 0:n], lhsT=wb[:, kc, 0:ncol], rhs=xmodT[:, kc, t0:t0 + n], start=(kc == 0), stop=(kc == 15)),
                         reads=[wk], writes=[("ps", pb)])
                ob = ofm[ev % 4]
                ok_ = ("ofm", ev % 4)
                if ev % 2 == 0:
                    S.op("act", lambda e, ob=ob, pst=pst, n=n, ncol=ncol: e.activation(out=ob[0:ncol, 0:n], in_=pst[0:ncol, 0:n], func=AF.Copy), reads=[("ps", pb)], writes=[ok_])
                else:
                    S.op("dve", lambda e, ob=ob, pst=pst, n=n, ncol=ncol: e.tensor_copy(out=ob[0:ncol, 0:n], in_=pst[0:ncol, 0:n]), reads=[("ps", pb)], writes=[ok_])
                dma("sp", PJ[c0:c0 + ncol, t0:t0 + n], ob[0:ncol, 0:n], reads=[ok_], writes=[("PJ", c0, bi)], semkey=ok_)
                ev += 1
        tm_tiles = [(O_Z + i * 256, 256, "z", i * 256) for i in range(8)] + [(O_GV, 256, "gv", 0), (O_DT, 64, "dt", 0)]
        for i, (c0, ncol, kind, d0) in enumerate(tm_tiles):
            wb = wtm[i % 2]
            wk = ("wtm", i % 2)
            dma("pool", wb[:, :, 0:ncol], win[:, c0:c0 + ncol].rearrange("(kc p) c -> p kc c", p=128), writes=[wk], semkey=wk)
            for t in range(NT):
                pb = 4 + (ev % 4)
                pst = PS[pb]
                for kc in range(16):
                    S.op("pe", lambda e, wb=wb, kc=kc, pst=pst, t=t, ncol=ncol: e.matmul(pst[:, 0:ncol], lhsT=xmodT[:, kc, t * 128:(t + 1) * 128], rhs=wb[:, kc, 0:ncol], start=(kc == 0), stop=(kc == 15)),
                         reads=[wk], writes=[("ps", pb)])
                if kind == "dt":
                    ob = otf[ev % 2]
                    ok_ = ("otf", ev % 2)
                    dst = DTR[t * 128:(t + 1) * 128, :]
                else:
                    ob = otm[ev % 2]
                    ok_ = ("otm", ev % 2)
                    dst = (ZT if kind == "z" else GV)[t * 128:(t + 1) * 128, d0:d0 + ncol]
                if ev % 2 == 0:
                    S.op("act", lambda e, ob=ob, pst=pst, ncol=ncol: e.activation(out=ob[:, 0:ncol], in_=pst[:, 0:ncol], func=AF.Copy), reads=[("ps", pb)], writes=[ok_])
                else:
                    S.op("dve", lambda e, ob=ob, pst=pst, ncol=ncol: e.tensor_copy(out=ob[:, 0:ncol], in_=pst[:, 0:ncol]), reads=[("ps", pb)], writes=[ok_])
                dma("sp", dst, ob[:, 0:ncol], reads=[ok_], writes=[(kind, t, d0)], semkey=ok_)
                ev += 1

        if stop == "C":
            break
        phase_D(S, AR, A_KEEP, PS, l, W, K, PJ, QA, KAN, KPE, VA, QB, KB, blocks, T, NT, dma,
                dict(onesb=onesb, RAb=RAb, RBb=RBb))
        if stop == "D":
            break
        phase_F(S, AR, A_KEEP, PS, l, PJ, QA, KAN, KPE, VA, QB, KB, GV, YG, blocks, T, NT, NTL, NTC, SEQ, CTX, ctx_out, dma,
                dict(onesb=onesb))
        if stop == "F":
            break
        phase_G(S, AR, A_KEEP, PS, l, W, K, PJ, ZT, DTR, XC, XBT, YF, YG, T, NT, NTL, NTC, SEQ, CTX, ctx_out, dma,
                dict(onesb=onesb, onesf=onesf, identb=identb, ident=ident, Uf=Uf, Lf=Lf, nmf=nmf, nmb=nmb))
        if stop == "G":
            break
        phase_H(S, AR, A_KEEP, PS, l, W, PJ, YG, xsrc, xdst, gbc, blocks, T, NT, NTL, SEQ, ctx_out, last, dma)
        if stop == "H":
            break

    S.barrier()

    esem = {(e, k_): es.enter_context(nc.semaphore("s_%s%d" % (e, k_))) for e in ENGS for k_ in range(NSEM_ROT)}
    dsem = [es.enter_context(nc.semaphore("d%d" % i)) for i in range(40)]
    run = S.emit(esem, dsem)
    block = es.enter_context(nc.Block())

    @block.tensor
    def _(e):
        run("pe", e)

    @block.scalar
    def _(e):
        run("act", e)

    @block.vector
    def _(e):
        run("dve", e)

    @block.gpsimd
    def _(e):
        run("pool", e)

    @block.sync
    def _(e):
        run("sp", e)

    es.close()
    global LAST_S
    LAST_S = S
    return nc


def phase_D(S, AR, A_KEEP, PS, l, W, K, PJ, QA, KAN, KPE, VA, QB, KB, blocks, T, NT, dma, C):
    onesb, RAb, RBb = C["onesb"], C["RAb"], C["RBb"]
    S.barrier()
    AR.off = A_KEEP
    wuq = AR.alloc([4, 1536], BF16)
    wukv = AR.alloc([2, 2048], BF16)
    gq_ = AR.alloc([4], F32)
    gkv = AR.alloc([2], F32)
    gqb = AR.alloc([1], F32)
    gkb = AR.alloc([1], F32)
    dma("pool", wuq, W["mla_w_uq"][l].rearrange("(kc p) c -> p kc c", p=128), writes=["wuq"], semkey="d0")
    dma("pool", wukv, W["mla_w_ukv"][l].rearrange("(kc p) c -> p kc c", p=128), writes=["wukv"], semkey="d1")
    dma("sp", gq_, W["mla_q_norm"][l].rearrange("(t p) -> p t", p=128), writes=["gq_"], semkey="d2", slow=True)
    dma("sp", gkv, W["mla_kv_norm"][l].rearrange("(t p) -> p t", p=128), writes=["gkv"], semkey="d3", slow=True)
    dma("sp", gqb, W["gqa_q_norm"][l].rearrange("(t p) -> p t", p=128), writes=["gqb"], semkey="d4", slow=True)
    dma("sp", gkb, W["gqa_k_norm"][l].rearrange("(t p) -> p t", p=128), writes=["gkb"], semkey="d5", slow=True)
    wukv4 = wukv.rearrange("p k (h c) -> p k h c", h=8)
    cosA = AR.alloc([512], F32)
    sinA = AR.alloc([512], F32)
    cosB = AR.alloc([512], F32)
    sinB = AR.alloc([512], F32)
    xin_ = AR.alloc([4, 512], BF16)
    sq = AR.alloc([4, 512], BF16)
    xn = AR.alloc([4, 512], BF16)
    r1 = AR.alloc([512], F32)
    gin = [AR.alloc([512], BF16) for _ in range(2)]
    gsq = [AR.alloc([512], BF16) for _ in range(2)]
    gr = [AR.alloc([512], F32) for _ in range(2)]
    qf = [AR.alloc([512], F32) for _ in range(2)]
    qb_ = [AR.alloc([512], BF16) for _ in range(2)]
    t1 = [AR.alloc([512], F32) for _ in range(2)]
    t2 = [AR.alloc([512], F32) for _ in range(2)]
    ob = [AR.alloc([512], BF16) for _ in range(4)]
    cnt = {"ob": 0, "ps": 0, "g": 0}

    def next_ob():
        i = cnt["ob"] % 4
        cnt["ob"] += 1
        return ob[i], ("ob", i)

    def next_ps():
        i = cnt["ps"] % 6
        cnt["ps"] += 1
        return PS[i], ("ps", i)

    def store(dst, src_ps, np_, n, wkey, eng):
        o_, ok_ = next_ob()
        if eng == "act":
            S.op("act", lambda e: e.activation(out=o_[0:np_, 0:n], in_=src_ps, func=AF.Copy), reads=[wkey], writes=[ok_])
        else:
            S.op("dve", lambda e: e.tensor_copy(out=o_[0:np_, 0:n], in_=src_ps), reads=[wkey], writes=[ok_])
        kx = key_dram(dst)
        dma("sp", dst, o_[0:np_, 0:n], reads=[ok_], writes=[kx], semkey=ok_)
        return kx

    kd = [0]

    def key_dram(dst):
        kd[0] += 1
        return ("dram", kd[0])

    def rms_rows(src_row0, nk, gain, n, t0, inv_n):
        dma("sp", xin_[:, 0:nk, 0:n], PJ[src_row0:src_row0 + nk * 128, t0:t0 + n].rearrange("(kt p) n -> p kt n", p=128),
            writes=["xin_"], semkey="xin_")
        S.op("act", lambda e: e.activation(out=sq[:, 0:nk, 0:n], in_=xin_[:, 0:nk, 0:n], func=AF.Square), reads=["xin_"], writes=["sq"])
        pst, pk = next_ps()
        for kt in range(nk):
            S.op("pe", lambda e, kt=kt: e.matmul(pst[:, 0:n], lhsT=onesb, rhs=sq[:, kt, 0:n], start=(kt == 0), stop=(kt == nk - 1)), reads=["sq"], writes=[pk])
        S.op("dve", lambda e: e.tensor_scalar(out=r1[:, 0:n], in0=pst[:, 0:n], scalar1=inv_n, scalar2=EPS, op0=ALU.mult, op1=ALU.add), reads=[pk], writes=["r1"])
        S.op("act", lambda e: e.activation(out=r1[:, 0:n], in_=r1[:, 0:n], func=AF.Sqrt), reads=["r1"], writes=["r1"])
        S.op("dve", lambda e: e.reciprocal(out=r1[:, 0:n], in_=r1[:, 0:n]), reads=["r1"], writes=["r1"])
        for kt in range(nk):
            S.op("dve", lambda e, kt=kt: e.scalar_tensor_tensor(out=xn[:, kt, 0:n], in0=xin_[:, kt, 0:n], scalar=gain[:, kt:kt + 1], in1=r1[:, 0:n], op0=ALU.mult, op1=ALU.mult),
                 reads=["xin_", "r1"], writes=["xn"])

    def rope(src_ps, pk, np_, n, Rm, cos_, sin_, dst):
        i = cnt["g"] % 2
        cnt["g"] += 1
        S.op("act", lambda e: e.activation(out=qf[i][0:np_, 0:n], in_=src_ps, func=AF.Copy), reads=[pk], writes=[("qf", i)])
        S.op("dve", lambda e: e.tensor_copy(out=qb_[i][0:np_, 0:n], in_=src_ps), reads=[pk], writes=[("qb", i)])
        rope_sb(i, np_, n, Rm, cos_, sin_, dst)

    def rope_sb(i, np_, n, Rm, cos_, sin_, dst, extra_reads=()):
        pst2, pk2 = next_ps()
        S.op("pe", lambda e: e.matmul(pst2[0:np_, 0:n], lhsT=Rm[0:np_, 0:np_], rhs=qb_[i][0:np_, 0:n], start=True, stop=True), reads=[("qb", i)], writes=[pk2])
        S.op("dve", lambda e: e.tensor_tensor(out=t1[i][0:np_, 0:n], in0=qf[i][0:np_, 0:n], in1=cos_[0:np_, 0:n], op=ALU.mult), reads=[("qf", i), "tab"], writes=[("t1", i)])
        S.op("dve", lambda e: e.tensor_tensor(out=t2[i][0:np_, 0:n], in0=pst2[0:np_, 0:n], in1=sin_[0:np_, 0:n], op=ALU.mult), reads=[pk2, "tab"], writes=[("t2", i)])
        o_, ok_ = next_ob()
        S.op("pool", lambda e: e.tensor_tensor(out=o_[0:np_, 0:n], in0=t1[i][0:np_, 0:n], in1=t2[i][0:np_, 0:n], op=ALU.add), reads=[("t1", i), ("t2", i)], writes=[ok_])
        dma("sp", dst, o_[0:np_, 0:n], reads=[ok_], writes=[key_dram(dst)] + list(extra_reads), semkey=ok_)

    def do_block(bi, t0, n):
        dma("sp", cosA[0:64, 0:n], K["k_cosA"][:, t0:t0 + n], writes=["tab"], semkey="tabA")
        dma("sp", sinA[0:64, 0:n], K["k_sinA"][:, t0:t0 + n], writes=["tab"], semkey="tabA2")
        dma("sp", cosB[:, 0:n], K["k_cosB"][:, t0:t0 + n], writes=["tab"], semkey="tabB")
        dma("sp", sinB[:, 0:n], K["k_sinB"][:, t0:t0 + n], writes=["tab"], semkey="tabB2")
        rms_rows(O_CQ, 4, gq_, n, t0, 1.0 / 512)
        if DSTOP == 11:
            return
        for h in range(8):
            pst, pk = next_ps()
            for kc in range(4):
                S.op("pe", lambda e, kc=kc, h=h, pst=pst: e.matmul(pst[:, 0:n], lhsT=wuq[:, kc, h * 192:h * 192 + 128], rhs=xn[:, kc, 0:n], start=(kc == 0), stop=(kc == 3)),
                     reads=["wuq", "xn"], writes=[pk])
            store(QA[h, 0:128, t0:t0 + n], pst[:, 0:n], 128, n, pk, "act" if h % 2 else "dve")
            if DSTOP == 12:
                continue
            pst, pk = next_ps()
            for kc in range(4):
                S.op("pe", lambda e, kc=kc, h=h, pst=pst: e.matmul(pst[0:64, 0:n], lhsT=wuq[:, kc, h * 192 + 128:h * 192 + 192], rhs=xn[:, kc, 0:n], start=(kc == 0), stop=(kc == 3)),
                     reads=["wuq", "xn"], writes=[pk])
            kx = store(QA[h, 128:192, t0:t0 + n], pst[0:64, 0:n], 64, n, pk, "dve" if h % 2 else "act")
            i = cnt["g"] % 2
            cnt["g"] += 1
            dma("sp", qb_[i][0:64, 0:n], QA[h, 128:192, t0:t0 + n], reads=[kx], writes=[("qb", i)], semkey=("qbl", i))
            S.op("act", lambda e, i=i: e.activation(out=qf[i][0:64, 0:n], in_=qb_[i][0:64, 0:n], func=AF.Copy), reads=[("qb", i)], writes=[("qf", i)])
            rope_sb(i, 64, n, RAb, cosA, sinA, QA[h, 128:192, t0:t0 + n], extra_reads=[kx])
        if DSTOP == 1:
            return
        rms_rows(O_CKV, 2, gkv, n, t0, 1.0 / 256)
        for h in range(8):
            pst, pk = next_ps()
            for kc in range(2):
                S.op("pe", lambda e, kc=kc, h=h, pst=pst: e.matmul(pst[:, 0:n], lhsT=wukv[:, kc, h * 256:h * 256 + 128], rhs=xn[:, kc, 0:n], start=(kc == 0), stop=(kc == 1)),
                     reads=["wukv", "xn"], writes=[pk])
            store(KAN[h, :, t0:t0 + n], pst[:, 0:n], 128, n, pk, "act" if h % 2 else "dve")
        for tt in range(n // 128):
            for half in range(2):
                pst, pk = next_ps()
                for hd in range(4):
                    for kc in range(2):
                        c0_ = (half * 4 + hd) * 256 + 128
                        S.op("pe", lambda e, kc=kc, hd=hd, c0_=c0_, tt=tt, pst=pst: e.matmul(pst[:, hd * 128:(hd + 1) * 128], lhsT=xn[:, kc, tt * 128:(tt + 1) * 128],
                                                                                     rhs=wukv[:, kc, c0_:c0_ + 128], start=(kc == 0), stop=(kc == 1)),
                             reads=["wukv", "xn"], writes=[pk])
                store(VA[t0 + tt * 128:t0 + (tt + 1) * 128, half * 512:(half + 1) * 512], pst[:, :], 128, 512, pk, "act" if half else "dve")
        if DSTOP == 2:
            return
        i = cnt["g"] % 2
        cnt["g"] += 1
        dma("sp", qb_[i][0:64, 0:n], PJ[O_KR:O_KR + 64, t0:t0 + n], writes=[("qb", i)], semkey=("qbl", i))
        S.op("act", lambda e, i=i: e.activation(out=qf[i][0:64, 0:n], in_=qb_[i][0:64, 0:n], func=AF.Copy), reads=[("qb", i)], writes=[("qf", i)])
        rope_sb(i, 64, n, RAb, cosA, sinA, KPE[:, t0:t0 + n])
        if DSTOP == 3:
            return
        for hh in range(10):
            row0 = O_GQ + hh * 128 if hh < 8 else O_GK + (hh - 8) * 128
            gain = gqb if hh < 8 else gkb
            dst = QB[hh, :, t0:t0 + n] if hh < 8 else KB[hh - 8, :, t0:t0 + n]
            j = hh % 2
            dma("sp", gin[j][:, 0:n], PJ[row0:row0 + 128, t0:t0 + n], writes=[("gin", j)], semkey=("gin", j))
            S.op("act", lambda e, j=j: e.activation(out=gsq[j][:, 0:n], in_=gin[j][:, 0:n], func=AF.Square), reads=[("gin", j)], writes=[("gsq", j)])
            pst, pk = next_ps()
            S.op("pe", lambda e, j=j, pst=pst: e.matmul(pst[:, 0:n], lhsT=onesb, rhs=gsq[j][:, 0:n], start=True, stop=True), reads=[("gsq", j)], writes=[pk])
            S.op("dve", lambda e, j=j, pst=pst: e.tensor_scalar(out=gr[j][:, 0:n], in0=pst[:, 0:n], scalar1=1.0 / 128, scalar2=EPS, op0=ALU.mult, op1=ALU.add), reads=[pk], writes=[("gr", j)])
            S.op("act", lambda e, j=j: e.activation(out=gr[j][:, 0:n], in_=gr[j][:, 0:n], func=AF.Sqrt), reads=[("gr", j)], writes=[("gr", j)])
            S.op("dve", lambda e, j=j: e.reciprocal(out=gr[j][:, 0:n], in_=gr[j][:, 0:n]), reads=[("gr", j)], writes=[("gr", j)])
            i = cnt["g"] % 2
            cnt["g"] += 1
            S.op("dve", lambda e, j=j, i=i, gain=gain: e.scalar_tensor_tensor(out=qf[i][:, 0:n], in0=gin[j][:, 0:n], scalar=gain[:, 0:1], in1=gr[j][:, 0:n], op0=ALU.mult, op1=ALU.mult),
                 reads=[("gin", j), ("gr", j)], writes=[("qf", i)])
            S.op("act", lambda e, i=i: e.activation(out=qb_[i][:, 0:n], in_=qf[i][:, 0:n], func=AF.Copy), reads=[("qf", i)], writes=[("qb", i)])
            rope_sb(i, 128, n, RBb, cosB, sinB, dst)

    for bi, (t0, n) in enumerate(blocks):
        do_block(bi, t0, n)


def phase_F(S, AR, A_KEEP, PS, l, PJ, QA, KAN, KPE, VA, QB, KB, GV, YG, blocks, T, NT, NTL, NTC, SEQ, CTX, ctx_out, dma, C):
    onesb = C["onesb"]
    S.barrier()
    AR.off = A_KEEP
    kpe = AR.alloc([T], BF16)
    kn = [AR.alloc([T], BF16) for _ in range(2)]
    vs = [AR.alloc([NT, 128], BF16) for _ in range(2)]
    qn = [AR.alloc([512], BF16) for _ in range(2)]
    qp = [AR.alloc([512], BF16) for _ in range(2)]
    pt = [AR.alloc([512], BF16) for _ in range(3)]
    gl = [AR.alloc([512], BF16) for _ in range(2)]
    gs = [AR.alloc([512], F32) for _ in range(2)]
    rinv = [AR.alloc([512], F32) for _ in range(2)]
    yo = [AR.alloc([512], F32) for _ in range(2)]
    yb = [AR.alloc([512], BF16) for _ in range(2)]
    dma("sp", kpe[0:64, :], KPE[:, :], writes=["kpe"], semkey="kpe")
    it = [0]
    sidx = [0]

    def do_q(kind, kv, kb_, h, bi, t0, n, scale):
        is_ctx = (t0 >= SEQ)
        kts = list(range(NTL, NT)) if is_ctx else list(range(NT))
        j = it[0] % 2
        it[0] += 1
        if kind == "A":
            dma("sp", qn[j][:, 0:n], QA[h, 0:128, t0:t0 + n], writes=[("qn", j)], semkey=("qn", j))
            dma("sp", qp[j][0:64, 0:n], QA[h, 128:192, t0:t0 + n], writes=[("qp", j)], semkey=("qp", j))
            grow = O_GA + h * 128
        else:
            dma("sp", qn[j][:, 0:n], QB[h, :, t0:t0 + n], writes=[("qn", j)], semkey=("qn", j))
            grow = O_GB + h * 128
        dma("sp", gl[j][:, 0:n], PJ[grow:grow + 128, t0:t0 + n], writes=[("gl", j)], semkey=("gl", j))
        S.op("act", lambda e: e.activation(out=gs[j][:, 0:n], in_=gl[j][:, 0:n], func=AF.Silu), reads=[("gl", j)], writes=[("gs", j)])
        po = PS[3 + j]
        pl = PS[5 + j]
        pok = ("ps", 3 + j)
        plk = ("ps", 5 + j)

        def s_mm(kt):
            si = sidx[0] % 3
            sidx[0] += 1
            pst = PS[si]
            if kind == "A":
                S.op("pe", lambda e: e.matmul(pst[:, 0:n], lhsT=kn[kb_][:, kt * 128:(kt + 1) * 128], rhs=qn[j][:, 0:n], start=True, stop=False),
                     reads=[("kn", kb_), ("qn", j)], writes=[("ps", si)])
                S.op("pe", lambda e: e.matmul(pst[:, 0:n], lhsT=kpe[0:64, kt * 128:(kt + 1) * 128], rhs=qp[j][0:64, 0:n], start=False, stop=True),
                     reads=["kpe", ("qp", j)], writes=[("ps", si)])
            else:
                S.op("pe", lambda e: e.matmul(pst[:, 0:n], lhsT=kn[kb_][:, kt * 128:(kt + 1) * 128], rhs=qn[j][:, 0:n], start=True, stop=True),
                     reads=[("kn", kb_), ("qn", j)], writes=[("ps", si)])
            return si

        def pv(kt, si, first, lastk):
            pst = PS[si]
            p_ = pt[si]
            S.op("act", lambda e: e.activation(out=p_[:, 0:n], in_=pst[:, 0:n], func=AF.Exp, scale=scale), reads=[("ps", si)], writes=[("pt", si)])
            S.op("pe", lambda e: e.matmul(po[:, 0:n], lhsT=vs[kb_][:, kt, :], rhs=p_[:, 0:n], start=first, stop=lastk),
                 reads=[("vs", kb_), ("pt", si)], writes=[pok])
            S.op("pe", lambda e: e.matmul(pl[:, 0:n], lhsT=onesb, rhs=p_[:, 0:n], start=first, stop=lastk),
                 reads=[("pt", si)], writes=[plk])

        pend = []
        for kt in kts:
            si = s_mm(kt)
            pend.append((kt, si))
            if len(pend) > 1:
                k0, s0 = pend.pop(0)
                pv(k0, s0, first=(k0 == kts[0]), lastk=False)
        k0, s0 = pend.pop(0)
        pv(k0, s0, first=(k0 == kts[0]), lastk=True)
        S.op("dve", lambda e: e.reciprocal(out=rinv[j][:, 0:n], in_=pl[:, 0:n]), reads=[plk], writes=[("rinv", j)])
        S.op("dve", lambda e: e.tensor_tensor(out=yo[j][:, 0:n], in0=po[:, 0:n], in1=rinv[j][:, 0:n], op=ALU.mult), reads=[pok, ("rinv", j)], writes=[("yo", j)])
        S.op("pool", lambda e: e.tensor_tensor(out=yb[j][:, 0:n], in0=yo[j][:, 0:n], in1=gs[j][:, 0:n], op=ALU.mult), reads=[("yo", j), ("gs", j)], writes=[("yb", j)])
        yrow = (0 if kind == "A" else 1024) + h * 128
        dma("sp", YG[yrow:yrow + 128, t0:t0 + n], yb[j][:, 0:n], reads=[("yb", j)], writes=[("YG", yrow, bi)], semkey=("yb", j))

    for kind in ("A", "B"):
        nkv = 8 if kind == "A" else 2
        grp = 1 if kind == "A" else 4
        scale = (192.0 if kind == "A" else 128.0) ** -0.5
        for kv in range(nkv):
            kb_ = kv % 2
            kk = ("kn", kb_)
            vk = ("vs", kb_)
            if kind == "A":
                dma("sp", kn[kb_], KAN[kv, :, :], writes=[kk], semkey=kk)
                for c0_ in range(0, NT, 8):
                    c1_ = min(NT, c0_ + 8)
                    dma("act", vs[kb_][:, c0_:c1_, :], VA[c0_ * 128:c1_ * 128, kv * 128:(kv + 1) * 128].rearrange("(t p) d -> p t d", p=128), writes=[vk], semkey=vk)
            else:
                dma("sp", kn[kb_], KB[kv, :, :], writes=[kk], semkey=kk)
                for c0_ in range(0, NT, 8):
                    c1_ = min(NT, c0_ + 8)
                    dma("act", vs[kb_][:, c0_:c1_, :], GV[c0_ * 128:c1_ * 128, kv * 128:(kv + 1) * 128].rearrange("(t p) d -> p t d", p=128), writes=[vk], semkey=vk)
            for g in range(grp):
                h = kv * grp + g
                for bi, (t0, n) in enumerate(blocks):
                    if t0 >= SEQ and not ctx_out:
                        continue
                    do_q(kind, kv, kb_, h, bi, t0, n, scale)


def phase_G(S, AR, A_KEEP, PS, l, W, K, PJ, ZT, DTR, XC, XBT, YF, YG, T, NT, NTL, NTC, SEQ, CTX, ctx_out, dma, C):
    onesf, identb, ident, Uf, Lf, nmf, nmb = C["onesf"], C["identb"], C["ident"], C["Uf"], C["Lf"], C["nmf"], C["nmb"]
    S.barrier()
    AR.off = A_KEEP
    cw = AR.alloc([24, 5], F32)
    cbias = AR.alloc([24], F32)
    for k_ in range(5):
        dma("sp", cw[:, :, k_], W["ssd_conv_w"][l, k_].rearrange("(t p) -> p t", p=128), writes=["cw"], semkey=("g0", k_), slow=True)
    dma("sp", cbias, W["ssd_conv_b"][l].rearrange("(t p) -> p t", p=128), writes=["cbias"], semkey="g1", slow=True)
    xi = [AR.alloc([T + 8], BF16) for _ in range(2)]
    acc = [AR.alloc([T], F32) for _ in range(2)]
    co = [AR.alloc([T], BF16) for _ in range(4)]
    to = [AR.alloc([512], BF16) for _ in range(2)]
    for b in range(2):
        S.op("pool", lambda e, b=b: e.memset(xi[b], 0.0), writes=[("xi", b)])
    segs = [(0, SEQ, 0), (SEQ, CTX, 4)]

    def conv_tile(ct):
        b = ct % 2
        row0 = O_XBC + ct * 128
        dma("sp", xi[b][:, 2:2 + SEQ], PJ[row0:row0 + 128, 0:SEQ], writes=[("xi", b)], semkey=("xi", b))
        dma("sp", xi[b][:, 6 + SEQ:6 + T], PJ[row0:row0 + 128, SEQ:T], writes=[("xi", b)], semkey=("xi2", b))
        for (t0, n, ex) in segs:
            S.op("dve", lambda e, t0=t0, n=n, ex=ex: e.tensor_scalar_mul(out=acc[b][:, t0:t0 + n], in0=xi[b][:, t0 + ex:t0 + ex + n], scalar1=cw[:, ct, 0:1]),
                 reads=[("xi", b), "cw"], writes=[("acc", b)])
            for k in range(1, 5):
                S.op("dve", lambda e, k=k, t0=t0, n=n, ex=ex: e.scalar_tensor_tensor(out=acc[b][:, t0:t0 + n], in0=xi[b][:, t0 + ex + k:t0 + ex + k + n], scalar=cw[:, ct, k:k + 1], in1=acc[b][:, t0:t0 + n], op0=ALU.mult, op1=ALU.add),
                     reads=[("xi", b), "cw"], writes=[("acc", b)])
        c = ct % 4
        S.op("act", lambda e: e.activation(out=co[c], in_=acc[b], func=AF.Silu, bias=cbias[:, ct:ct + 1]), reads=[("acc", b), "cbias"], writes=[("co", c)])
        if ct >= 16:
            dma("act", XBT[(ct - 16) * 128:(ct - 15) * 128, :], co[c], reads=[("co", c)], writes=[("XBT", ct)], semkey=("co", c))

    def transp_group(q, t, k):
        pb = k % 2
        pstb = PS[pb][:, :].bitcast(BF16)
        for j in range(4):
            S.op("pe", lambda e, j=j: e.transpose(out=pstb[:, j * 128:(j + 1) * 128], in_=co[j][:, t * 128:(t + 1) * 128], identity=identb),
                 reads=[("co", j)], writes=[("ps", pb)])
        tb = k % 2
        if k % 2 == 0:
            S.op("act", lambda e: e.activation(out=to[tb], in_=pstb[:, 0:512], func=AF.Copy), reads=[("ps", pb)], writes=[("to", tb)])
        else:
            S.op("dve", lambda e: e.tensor_copy(out=to[tb], in_=pstb[:, 0:512]), reads=[("ps", pb)], writes=[("to", tb)])
        dma("sp", XC[t * 128:(t + 1) * 128, q * 512:(q + 1) * 512], to[tb], reads=[("to", tb)], writes=[("XC", q, t)], semkey=("to", tb))

    kk = 0
    for q in range(6):
        for j in range(4):
            conv_tile(q * 4 + j)
        for t in range(NT):
            transp_group(q, t, kk)
            kk += 1

    if DSTOP == 21:
        return
    S.barrier()
    AR.off = A_KEEP
    abc = AR.alloc([64], F32)
    dtb = AR.alloc([64], F32)
    dsk = AR.alloc([32, 1], F32)
    gnorm = AR.alloc([2048], F32)
    hT = AR.alloc([4, 512], F32)
    hb = AR.alloc([4, 512], BF16)
    dma("sp", abc, W["ssd_a_log"][l:l + 1, :].partition_broadcast(128), writes=["abc"], semkey="g2")
    dma("sp", dtb, W["ssd_dt_bias"][l:l + 1, :].partition_broadcast(128), writes=["dtb"], semkey="g3")
    dma("sp", dsk[:, :, 0], W["ssd_d"][l:l + 1, :].partition_broadcast(128), writes=["dsk"], semkey="g4")
    dma("sp", gnorm, W["ssd_norm"][l:l + 1, :].partition_broadcast(128), writes=["gnorm"], semkey="g5")
    S.op("act", lambda e: e.activation(out=abc, in_=abc, func=AF.Exp), reads=["abc"], writes=["abc"])
    S.op("dve", lambda e: e.tensor_scalar_mul(out=abc, in0=abc, scalar1=-1.0), reads=["abc"], writes=["abc"])
    xc = [AR.alloc([3072], BF16) for _ in range(2)]
    dtr = [AR.alloc([64], F32) for _ in range(2)]
    bct = [AR.alloc([8, 128], BF16) for _ in range(2)]
    dt_ = AR.alloc([32, 1], F32)
    dta = AR.alloc([32], F32)
    Acol = AR.alloc([32, 1], F32)
    dte = AR.alloc([32, 1], F32)
    eA = AR.alloc([32, 1], F32)
    dec = AR.alloc([32, 1], F32)
    tmp32 = AR.alloc([32], F32)
    atot = AR.alloc([32], F32)
    xdtf = AR.alloc([32, 64], F32)
    xdtb = AR.alloc([32, 64], BF16)
    xdte = AR.alloc([32, 64], BF16)
    cbT = AR.alloc([4, 128], F32)
    seg = [AR.alloc([4, 128], F32) for _ in range(2)]
    Em = [AR.alloc([4, 128], F32) for _ in range(2)]
    MT = [AR.alloc([4, 128], BF16) for _ in range(2)]
    yoff = AR.alloc([8, 64], F32)
    ych = [AR.alloc([2048], F32) for _ in range(2)]
    yfl = AR.alloc([2048], F32)
    zt = AR.alloc([2048], BF16)
    zs = AR.alloc([2048], F32)
    junk = AR.alloc([512], F32)
    ss = AR.alloc([4], F32)
    vn = AR.alloc([2048], BF16)
    ygs = [AR.alloc([4, 128], BF16) for _ in range(2)]
    cnt = {"c": 0, "s": 0, "y": 0}

    def chunk(tt, d, want_y):
        b = cnt["c"] % 2
        cnt["c"] += 1
        Tri = Uf if d == 0 else Lf
        nm = nmf if d == 0 else nmb
        dma("sp", xc[b], XC[tt * 128:(tt + 1) * 128, :], writes=[("xc", b)], semkey=("xc", b))
        dma("sp", dtr[b], DTR[tt * 128:(tt + 1) * 128, :], writes=[("dtr", b)], semkey=("dtr", b))
        dma("act", bct[b], XBT[:, tt * 128:(tt + 1) * 128].rearrange("(r p) t -> p r t", p=128), writes=[("bct", b)], semkey=("bct", b))
        xs3 = xc[b][:, 0:2048].rearrange("p (h c) -> p h c", h=32)
        S.op("dve", lambda e: e.tensor_tensor(out=dt_[:, :, 0], in0=dtr[b][:, d * 32:(d + 1) * 32], in1=dtb[:, d * 32:(d + 1) * 32], op=ALU.add), reads=[("dtr", b), "dtb"], writes=["dt"])
        S.op("act", lambda e: e.activation(out=dt_, in_=dt_, func=AF.Exp), reads=["dt"], writes=["dt"])
        S.op("dve", lambda e: e.tensor_scalar_add(out=dt_, in0=dt_, scalar1=1.0), reads=["dt"], writes=["dt"])
        S.op("act", lambda e: e.activation(out=dt_, in_=dt_, func=AF.Ln), reads=["dt"], writes=["dt"])
        S.op("dve", lambda e: e.tensor_tensor(out=dta, in0=dt_[:, :, 0], in1=abc[:, d * 32:(d + 1) * 32], op=ALU.mult), reads=["dt", "abc"], writes=["dta"])
        pa = PS[0]
        S.op("pe", lambda e: e.matmul(pa[:, 0:32], lhsT=Tri, rhs=dta, start=True, stop=True), reads=["dta"], writes=[("ps", 0)])
        S.op("pe", lambda e: e.matmul(pa[:, 32:64], lhsT=onesf, rhs=dta, start=True, stop=True), reads=["dta"], writes=[("ps", 0)])
        S.op("act", lambda e: e.activation(out=Acol[:, :, 0], in_=pa[:, 0:32], func=AF.Copy), reads=[("ps", 0)], writes=["Acol"])
        S.op("act", lambda e: e.activation(out=atot, in_=pa[:, 32:64], func=AF.Copy), reads=[("ps", 0)], writes=["atot"])
        S.op("dve", lambda e: e.tensor_tensor(out=tmp32, in0=atot, in1=Acol[:, :, 0], op=ALU.subtract), reads=["atot", "Acol"], writes=["tmp32"])
        S.op("act", lambda e: e.activation(out=dte[:, :, 0], in_=tmp32, func=AF.Exp), reads=["tmp32"], writes=["dte"])
        S.op("act", lambda e: e.activation(out=eA[:, :, 0], in_=Acol[:, :, 0], func=AF.Exp), reads=["Acol"], writes=["eA"])
        S.op("act", lambda e: e.activation(out=dec[:, :, 0], in_=pa[:, 32:64], func=AF.Exp), reads=[("ps", 0)], writes=["dec"])
        S.op("dve", lambda e: e.tensor_tensor(out=xdtf, in0=xs3, in1=dt_.to_broadcast([128, 32, 64]), op=ALU.mult), reads=[("xc", b), "dt"], writes=["xdtf"])
        S.op("pool", lambda e: e.tensor_copy(out=xdtb, in_=xdtf), reads=["xdtf"], writes=["xdtb"])
        S.op("dve", lambda e: e.tensor_tensor(out=xdte, in0=xdtf, in1=dte.to_broadcast([128, 32, 64]), op=ALU.mult), reads=["xdtf", "dte"], writes=["xdte"])
        pcb = PS[1]
        for g in range(4):
            S.op("pe", lambda e, g=g: e.matmul(pcb[:, g * 128:(g + 1) * 128], lhsT=bct[b][:, g, :], rhs=bct[b][:, 4 + g, :], start=True, stop=True),
                 reads=[("bct", b)], writes=[("ps", 1)])
        S.op("act", lambda e: e.activation(out=cbT, in_=pcb[:, :].rearrange("p (g c) -> p g c", g=4), func=AF.Copy), reads=[("ps", 1)], writes=["cbT"])
        yb_ = cnt["y"] % 2
        cnt["y"] += 1
        y_ = ych[yb_]
        for g in range(4):
            py = PS[4 + (g % 2)]
            pyk = ("ps", 4 + (g % 2))
            if want_y:
                for half in range(2):
                    si = cnt["s"] % 2
                    cnt["s"] += 1
                    psg = PS[2 + si]
                    psk = ("ps", 2 + si)
                    h0 = g * 8 + half * 4
                    S.op("pe", lambda e, psg=psg: e.matmul(psg[:, :], lhsT=ident, rhs=nm, start=True, stop=False), writes=[psk])
                    for hh in range(4):
                        S.op("pe", lambda e, psg=psg, hh=hh, h0=h0: e.matmul(psg[:, hh * 128:(hh + 1) * 128], lhsT=dta[:, h0 + hh:h0 + hh + 1].to_broadcast([128, 128]), rhs=Tri, start=False, stop=(hh == 3)),
                             reads=["dta"], writes=[psk])
                    S.op("dve", lambda e, psg=psg, si=si, h0=h0: e.tensor_tensor(out=seg[si], in0=psg[:, :].rearrange("p (a c) -> p a c", a=4), in1=Acol[:, h0:h0 + 4, :].to_broadcast([128, 4, 128]), op=ALU.subtract),
                         reads=[psk, "Acol"], writes=[("seg", si)])
                    S.op("act", lambda e, si=si: e.activation(out=Em[si], in_=seg[si], func=AF.Exp), reads=[("seg", si)], writes=[("Em", si)])
                    S.op("dve", lambda e, si=si, g=g: e.tensor_tensor(out=MT[si], in0=Em[si], in1=cbT[:, g:g + 1, :].to_broadcast([128, 4, 128]), op=ALU.mult),
                         reads=[("Em", si), "cbT"], writes=[("MT", si)])
                    for hh in range(4):
                        e_ = half * 4 + hh
                        S.op("pe", lambda e, si=si, hh=hh, e_=e_, py=py, h0=h0: e.matmul(py[:, e_ * 64:(e_ + 1) * 64], lhsT=MT[si][:, hh, :], rhs=xdtb[:, h0 + hh, :], start=True, stop=True),
                             reads=[("MT", si), "xdtb"], writes=[pyk])
                po = PS[6]
                S.op("pe", lambda e, g=g, po=po: e.matmul(po[:, :], lhsT=bct[b][:, 4 + g, :], rhs=hb[:, g, :], start=True, stop=True), reads=[("bct", b), "hb"], writes=[("ps", 6)])
                S.op("dve", lambda e, g=g, po=po: e.tensor_tensor(out=yoff, in0=po[:, :].rearrange("p (a c) -> p a c", a=8), in1=eA[:, g * 8:(g + 1) * 8, :].to_broadcast([128, 8, 64]), op=ALU.mult),
                     reads=[("ps", 6), "eA"], writes=["yoff"])
                S.op("dve", lambda e, g=g, py=py: e.tensor_tensor(out=y_[:, g * 512:(g + 1) * 512], in0=py[:, :], in1=yoff.rearrange("p a c -> p (a c)"), op=ALU.add),
                     reads=[pyk, "yoff"], writes=[("ych", yb_)])
            pst_ = PS[7]
            S.op("pe", lambda e, g=g, pst_=pst_: e.matmul(pst_[:, :], lhsT=xc[b][:, 2048 + g * 128:2048 + (g + 1) * 128], rhs=xdte[:, g * 8:(g + 1) * 8, :], start=True, stop=True),
                 reads=[("xc", b), "xdte"], writes=[("ps", 7)])
            S.op("dve", lambda e, g=g: e.tensor_tensor(out=hT[:, g, :].rearrange("p (a c) -> p a c", a=8), in0=hT[:, g, :].rearrange("p (a c) -> p a c", a=8), in1=dec[:, g * 8:(g + 1) * 8, :].to_broadcast([128, 8, 64]), op=ALU.mult),
                 reads=["hT", "dec"], writes=["hT"])
            S.op("dve", lambda e, g=g, pst_=pst_: e.tensor_tensor(out=hT[:, g, :], in0=pst_[:, :], in1=hT[:, g, :], op=ALU.add), reads=[("ps", 7), "hT"], writes=["hT"])
            S.op("pool", lambda e, g=g: e.tensor_copy(out=hb[:, g, :], in_=hT[:, g, :]), reads=["hT"], writes=["hb"])
        return b, yb_

    order_f = list(range(NTL, NT)) + list(range(NTL))
    order_b = list(range(NT - 1, NTL - 1, -1)) + list(range(NTL - 1, -1, -1))
    S.op("dve", lambda e: e.memset(hT, 0.0), writes=["hT"])
    S.op("pool", lambda e: e.memset(hb, 0.0), writes=["hb"])
    for tt in order_f:
        want = (tt < NTL) or ctx_out
        b, yb_ = chunk(tt, 0, want)
        if want:
            dma("sp", YF[tt * 128:(tt + 1) * 128, :], ych[yb_], reads=[("ych", yb_)], writes=[("YF", tt)], semkey=("ych", yb_))
    S.op("dve", lambda e: e.memset(hT, 0.0), reads=["hb"], writes=["hT"])
    S.op("pool", lambda e: e.memset(hb, 0.0), writes=["hb"])

    def finish(tt, b, yb_):
        y_ = ych[yb_]
        dma("sp", yfl, YF[tt * 128:(tt + 1) * 128, :], reads=[("YF", tt)], writes=["yfl"], semkey="yfl")
        dma("act", zt, ZT[tt * 128:(tt + 1) * 128, :], writes=["zt"], semkey="zt")
        xs3 = xc[b][:, 0:2048].rearrange("p (h c) -> p h c", h=32)
        S.op("pool", lambda e: e.tensor_tensor(out=y_, in0=y_, in1=yfl, op=ALU.add), reads=[("ych", yb_), "yfl"], writes=[("ych", yb_)])
        S.op("dve", lambda e: e.tensor_tensor(out=xdtf, in0=xs3, in1=dsk.to_broadcast([128, 32, 64]), op=ALU.mult), reads=[("xc", b), "dsk"], writes=["xdtf"])
        S.op("dve", lambda e: e.tensor_tensor(out=y_, in0=y_, in1=xdtf.rearrange("p a c -> p (a c)"), op=ALU.add), reads=[("ych", yb_), "xdtf"], writes=[("ych", yb_)])
        S.op("act", lambda e: e.activation(out=zs, in_=zt, func=AF.Silu), reads=["zt"], writes=["zs"])
        S.op("dve", lambda e: e.tensor_tensor(out=y_, in0=y_, in1=zs, op=ALU.mult), reads=[("ych", yb_), "zs"], writes=[("ych", yb_)])
        S.op("dve", lambda e: e.memset(ss, 0.0), writes=["ss"])
        for g in range(4):
            S.op("act", lambda e, g=g: e.activation(out=junk, in_=y_[:, g * 512:(g + 1) * 512], func=AF.Square, accum_out=ss[:, g:g + 1]), reads=[("ych", yb_), "ss"], writes=["ss", "junk"])
        S.op("dve", lambda e: e.tensor_scalar(out=ss, in0=ss, scalar1=1.0 / 512, scalar2=EPS, op0=ALU.mult, op1=ALU.add), reads=["ss"], writes=["ss"])
        S.op("act", lambda e: e.activation(out=ss, in_=ss, func=AF.Sqrt), reads=["ss"], writes=["ss"])
        S.op("dve", lambda e: e.reciprocal(out=ss, in_=ss), reads=["ss"], writes=["ss"])
        for g in range(4):
            S.op("dve", lambda e, g=g: e.scalar_tensor_tensor(out=vn[:, g * 512:(g + 1) * 512], in0=y_[:, g * 512:(g + 1) * 512], scalar=ss[:, g:g + 1], in1=gnorm[:, g * 512:(g + 1) * 512], op0=ALU.mult, op1=ALU.mult),
                 reads=[("ych", yb_), "ss", "gnorm"], writes=["vn"])
        for q in range(4):
            pb = q % 2
            pstb = PS[pb][:, :].bitcast(BF16)
            for j in range(4):
                kc = q * 4 + j
                S.op("pe", lambda e, j=j, kc=kc, pstb=pstb: e.transpose(out=pstb[:, j * 128:(j + 1) * 128], in_=vn[:, kc * 128:(kc + 1) * 128], identity=identb), reads=["vn"], writes=[("ps", pb)])
            S.op("act", lambda e, q=q, pstb=pstb: e.activation(out=ygs[q % 2], in_=pstb[:, 0:512].rearrange("p (a c) -> p a c", a=4), func=AF.Copy), reads=[("ps", pb)], writes=[("ygs", q % 2)])
            dma("sp", YG[2048 + q * 512:2048 + (q + 1) * 512, tt * 128:(tt + 1) * 128].rearrange("(j p) t -> p j t", p=128), ygs[q % 2], reads=[("ygs", q % 2)], writes=[("YGc", q, tt)], semkey=("ygs", q % 2))

    for tt in order_b:
        want = (tt < NTL) or ctx_out
        b, yb_ = chunk(tt, 1, want)
        if want:
            finish(tt, b, yb_)


def phase_H(S, AR, A_KEEP, PS, l, W, PJ, YG, xsrc, xdst, gbc, blocks, T, NT, NTL, SEQ, ctx_out, last, dma):
    S.barrier()
    AR.off = A_KEEP
    NB = 256
    lng = AR.alloc([2048], F32)
    lnb = AR.alloc([2048], F32)
    dma("sp", lng, W["ln_g"][l:l + 1, :].partition_broadcast(128), writes=["lng"], semkey="h0")
    dma("sp", lnb, W["ln_b"][l:l + 1, :].partition_broadcast(128), writes=["lnb"], semkey="h1")
    ygb = AR.alloc([32, NB], BF16)
    wbr = [AR.alloc([32, 128], BF16) for _ in range(3)]
    mT = AR.alloc([16, NB], BF16)
    wo = [AR.alloc([16, 512], BF16) for _ in range(2)]
    xt = [AR.alloc([2048], F32) for _ in range(2)]
    mg = [AR.alloc([3, NB], BF16) for _ in range(2)]
    sg = [AR.alloc([3, NB], F32) for _ in range(2)]
    ta = [AR.alloc([3, NB], F32) for _ in range(2)]
    tg = AR.alloc([512], F32)
    stats = AR.alloc([4, 6], F32)
    mv = AR.alloc([2], F32)
    rstd = AR.alloc([1], F32)
    wbrv = W["w_br"][l]
    wov = W["w_out"][l]
    mgv = PJ[O_MG:O_MG + 6144, :].rearrange("(br f p) t -> p br f t", br=3, p=128)
    ntok = T if ctx_out else SEQ
    cnt = {"w": 0, "o": 0}
    for t0 in range(0, ntok, NB):
        r = 0 if t0 < SEQ else 1
        for k0_ in range(0, 32, 8):
            dma("sp", ygb[:, k0_:k0_ + 8, :], YG[k0_ * 128:(k0_ + 8) * 128, t0:t0 + NB].rearrange("(k p) t -> p k t", p=128), writes=["ygb"], semkey="ygb")
        for tt in range(2):
            dma("act", xt[tt], xsrc[t0 + tt * 128:t0 + (tt + 1) * 128, :], writes=[("xt", tt)], semkey=("xt", tt))
        for f in range(16):
            wi = cnt["w"] % 3
            cnt["w"] += 1
            wk = ("wbr", wi)
            dma("pool", wbr[wi], wbrv[:, f * 128:(f + 1) * 128].rearrange("(k p) c -> p k c", p=128), writes=[wk], semkey=wk)
            m = f % 2
            dma("sp", mg[m], mgv[:, :, f, t0:t0 + NB], writes=[("mg", m)], semkey=("mg", m))
            S.op("act", lambda e, m=m: e.activation(out=sg[m], in_=mg[m], func=AF.Sigmoid), reads=[("mg", m)], writes=[("sg", m)])
            for br, (k0, k1) in enumerate(((0, 8), (8, 16), (16, 32))):
                pb = (f % 2) * 3 + br
                for k in range(k0, k1):
                    S.op("pe", lambda e, wi=wi, k=k, pb=pb, k0=k0, k1=k1: e.matmul(PS[pb][:, 0:NB], lhsT=wbr[wi][:, k, :], rhs=ygb[:, k, :], start=(k == k0), stop=(k == k1 - 1)),
                         reads=[wk, "ygb"], writes=[("ps", pb)])
                S.op("dve", lambda e, m=m, br=br, pb=pb: e.tensor_tensor(out=ta[m][:, br, :], in0=PS[pb][:, 0:NB], in1=sg[m][:, br, :], op=ALU.mult),
                     reads=[("ps", pb), ("sg", m)], writes=[("ta", m)])
            S.op("pool", lambda e, m=m: e.tensor_tensor(out=ta[m][:, 0, :], in0=ta[m][:, 0, :], in1=ta[m][:, 1, :], op=ALU.add), reads=[("ta", m)], writes=[("ta", m)])
            S.op("pool", lambda e, m=m, f=f: e.tensor_tensor(out=mT[:, f, :], in0=ta[m][:, 0, :], in1=ta[m][:, 2, :], op=ALU.add), reads=[("ta", m)], writes=["mT"])
        for cb in range(4):
            oi = cnt["o"] % 2
            cnt["o"] += 1
            ok_ = ("wo", oi)
            dma("pool", wo[oi], wov[:, cb * 512:(cb + 1) * 512].rearrange("(f p) c -> p f c", p=128), writes=[ok_], semkey=ok_)
            for tt in range(2):
                pb = 6 + tt
                for f in range(16):
                    S.op("pe", lambda e, oi=oi, f=f, tt=tt, pb=pb: e.matmul(PS[pb][:, :], lhsT=mT[:, f, tt * 128:(tt + 1) * 128], rhs=wo[oi][:, f, :], start=(f == 0), stop=(f == 15)),
                         reads=[ok_, "mT"], writes=[("ps", pb)])
                S.op("dve", lambda e, pb=pb, cb=cb, r=r: e.tensor_tensor(out=tg, in0=PS[pb][:, :], in1=gbc[:, r, cb * 512:(cb + 1) * 512], op=ALU.mult), reads=[("ps", pb)], writes=["tg"])
                S.op("dve", lambda e, tt=tt, cb=cb: e.scalar_tensor_tensor(out=xt[tt][:, cb * 512:(cb + 1) * 512], in0=xt[tt][:, cb * 512:(cb + 1) * 512], scalar=ALPHA, in1=tg, op0=ALU.mult, op1=ALU.add),
                     reads=[("xt", tt), "tg"], writes=[("xt", tt)])
        for tt in range(2):
            xk = ("xt", tt)
            for c in range(4):
                S.op("dve", lambda e, tt=tt, c=c: e.bn_stats(out=stats[:, c, :], in_=xt[tt][:, c * 512:(c + 1) * 512]), reads=[xk], writes=["st"])
            S.op("dve", lambda e: e.bn_aggr(out=mv, in_=stats), reads=["st"], writes=["mv"])
            S.op("dve", lambda e: e.tensor_scalar_add(out=rstd, in0=mv[:, 1:2], scalar1=EPS), reads=["mv"], writes=["rs"])
            S.op("act", lambda e: e.activation(out=rstd, in_=rstd, func=AF.Sqrt), reads=["rs"], writes=["rs"])
            S.op("dve", lambda e: e.reciprocal(out=rstd, in_=rstd), reads=["rs"], writes=["rs"])
            S.op("dve", lambda e, tt=tt: e.tensor_scalar(out=xt[tt], in0=xt[tt], scalar1=mv[:, 0:1], scalar2=rstd[:, 0:1], op0=ALU.subtract, op1=ALU.mult), reads=[xk, "mv", "rs"], writes=[xk])
            S.op("pool", lambda e, tt=tt: e.tensor_tensor(out=xt[tt], in0=xt[tt], in1=lng, op=ALU.mult), reads=[xk, "lng"], writes=[xk])
            S.op("pool", lambda e, tt=tt: e.tensor_tensor(out=xt[tt], in0=xt[tt], in1=lnb, op=ALU.add), reads=[xk, "lnb"], writes=[xk])
            dma("sp", xdst[t0 + tt * 128:t0 + (tt + 1) * 128, :], xt[tt], reads=[xk], writes=[("xo", t0, tt)], semkey=("xts", tt))


_CACHE = {}


def _prep_inputs(inp, b, seq, ctx):
    m = {
        "xin": np.ascontiguousarray(np.concatenate([inp["x"][b], inp["ctx"][b]], 0)),
        "c2": np.ascontiguousarray(np.stack([inp["c"][b], inp["c_ctx"]], 0)),
        "ssd_a_log": np.ascontiguousarray(inp["ssd_a_log"].reshape(inp["ssd_a_log"].shape[0], 64)),
        "ssd_dt_bias": np.ascontiguousarray(inp["ssd_dt_bias"].reshape(inp["ssd_dt_bias"].shape[0], 64)),
        "w_br": np.ascontiguousarray(np.concatenate([inp["w_br_a"], inp["w_br_b"], inp["w_br_c"]], 1)),
    }
    for k in ("w_mod", "b_mod", "w_in", "mla_q_norm", "mla_w_uq", "mla_kv_norm", "mla_w_ukv", "gqa_q_norm", "gqa_k_norm",
              "ssd_conv_w", "ssd_conv_b", "ssd_d", "ssd_norm", "w_out", "ln_g", "ln_b"):
        m[k] = np.ascontiguousarray(inp[k])
    return m


def kernel(**inputs):
    inp = {k: np.asarray(v, dtype=np.float32) for k, v in inputs.items()}
    B, SEQ, _ = inp["x"].shape
    CTX = inp["ctx"].shape[1]
    depth = inp["w_in"].shape[0]
    keyc = (SEQ, CTX, depth)
    if keyc not in _CACHE:
        _CACHE[keyc] = build_program(SEQ, CTX, depth)
    nc = _CACHE[keyc]
    consts = host_consts(SEQ, CTX)
    shared = _prep_inputs(inp, 0, SEQ, CTX)
    in_maps = []
    ncores = 8
    for core in range(ncores):
        b = core % B
        m = dict(shared)
        m["xin"] = np.ascontiguousarray(np.concatenate([inp["x"][b], inp["ctx"][b]], 0))
        m["c2"] = np.ascontiguousarray(np.stack([inp["c"][b], inp["c_ctx"]], 0))
        m.update(consts)
        in_maps.append(m)
    res = run_bass_kernel_spmd(nc, in_maps, core_ids=list(range(ncores)))
    return np.stack([np.asarray(res.results[b]["out"], dtype=np.float32) for b in range(B)], 0)
```

```python
import math
from contextlib import ExitStack

import numpy as np
import concourse.bass as bass
import concourse.mybir as mybir
from concourse.bass_utils import run_bass_kernel_spmd

F32 = mybir.dt.float32
BF16 = mybir.dt.bfloat16
U8 = mybir.dt.uint8
AF = mybir.ActivationFunctionType
ALU = mybir.AluOpType

D = 2048
DEPTH = 2
GRID_W = 64
EPS = 1e-6
SPL = (512, 256, 64, 1024, 1024, 256, 256, 1024, 2048, 3072, 64, 6144)
OFF = [0]
for _s in SPL:
    OFF.append(OFF[-1] + _s)
(O_CQ, O_CKV, O_KR, O_GA, O_GQ, O_GK, O_GV, O_GB, O_Z, O_XBC, O_DT, O_MG) = OFF[:12]
INW = OFF[12]
ALPHA = (2 * DEPTH) ** 0.25
ENGS = ("pe", "act", "dve", "pool", "sp")
import os as _os
DSTOP = int(_os.environ.get("DSTOP", "0"))
NSEM_ROT = 6


class _Op:
    __slots__ = ("eng", "fn", "deps", "dma", "semkey", "idx", "sig", "sigval", "cidx", "epoch")

    def __init__(self, eng, fn, dma, semkey):
        self.eng = eng
        self.fn = fn
        self.deps = []
        self.dma = dma
        self.semkey = semkey
        self.sig = False
        self.sigval = None


class Sched:
    def __init__(self, nc):
        self.nc = nc
        self.q = {e: [] for e in ENGS}
        self.lastw = {}
        self.readers = {}
        self.ccount = {}
        self.lastdma = {}
        self.all = []
        self.epoch = 0

    def op(self, eng, fn, reads=(), writes=(), dma=False, semkey=None, extra=()):
        o = _Op(eng, fn, dma, semkey)
        o.idx = len(self.q[eng])
        o.epoch = self.epoch
        o.cidx = self.ccount.get(eng, 0)
        if not dma:
            self.ccount[eng] = o.cidx + 1
        deps = list(extra)
        for r in reads:
            w = self.lastw.get(r)
            if w is not None:
                deps.append(w)
        for w_ in writes:
            w = self.lastw.get(w_)
            if w is not None:
                deps.append(w)
            deps.extend(self.readers.get(w_, ()))
        seen = set()
        for d in deps:
            if d is o or id(d) in seen:
                continue
            seen.add(id(d))
            if (not d.dma) and d.eng == eng and not dma:
                if eng == "pe" or d.cidx < o.cidx - 1:
                    continue
            o.deps.append(d)
            d.sig = True
        for r in reads:
            lst = self.readers.setdefault(r, [])
            if not dma:
                for i_, x_ in enumerate(lst):
                    if (not x_.dma) and x_.eng == eng:
                        lst[i_] = o
                        break
                else:
                    lst.append(o)
            else:
                lst.append(o)
        for w_ in writes:
            self.lastw[w_] = o
            self.readers[w_] = []
        self.q[eng].append(o)
        self.all.append(o)
        if dma:
            o.semkey = (self.epoch, semkey)
            self.lastdma[semkey] = o
        return o

    def barrier(self):
        last = []
        for e in ENGS:
            for o in reversed(self.q[e]):
                if not o.dma:
                    last.append(o)
                    break
        last.extend(self.lastdma.values())
        self.lastdma = {}
        for e in ENGS:
            self.op(e, lambda eng: eng.nop() if hasattr(eng, "nop") else eng.engine_nop(), extra=last)
        self.lastw = {}
        self.readers = {}
        self.epoch += 1

    def emit(self, esem, dsem):
        keymap = {}
        semcnt = [0] * len(dsem)
        nper = {}
        cnt = {}
        for o in self.all:
            if o.dma:
                k = o.semkey
                if k not in keymap:
                    i = nper.get(k[0], 0)
                    nper[k[0]] = i + 1
                    if i >= len(dsem):
                        raise RuntimeError("out of dma semaphores")
                    keymap[k] = i
                i = keymap[k]
                semcnt[i] += 16
                o.sigval = (dsem[i], semcnt[i])
            elif o.sig:
                kk_ = (o.eng, o.epoch % NSEM_ROT)
                cnt[kk_] = cnt.get(kk_, 0) + 1
                o.sigval = (esem[kk_], cnt[kk_])

        def run(eng_name, eng):
            waited = {}
            for o in self.q[eng_name]:
                need = {}
                for d in o.deps:
                    s, v = d.sigval
                    key = id(s)
                    if waited.get(key, 0) >= v:
                        continue
                    if key not in need or need[key][1] < v:
                        need[key] = (s, v)
                for key, (s, v) in need.items():
                    eng.wait_ge(s, v)
                    waited[key] = v
                ins = o.fn(eng)
                if o.dma:
                    ins.then_inc(o.sigval[0], 16)
                elif o.sig:
                    ins.then_inc(o.sigval[0], 1)

        return run


class Arena:
    def __init__(self, ap_u8, size):
        self.ap = ap_u8
        self.size = size
        self.off = 0

    def reset(self):
        self.off = 0

    def alloc(self, free_shape, dt):
        esz = 4 if dt == F32 else 2
        n = 1
        for s in free_shape:
            n *= s
        nb = (n * esz + 63) // 64 * 64
        if self.off + nb > self.size:
            raise RuntimeError("arena overflow: %d + %d > %d" % (self.off, nb, self.size))
        v = self.ap[:, self.off:self.off + n * esz].bitcast(dt)
        self.off += nb
        if len(free_shape) == 2:
            v = v.rearrange("p (a b) -> p a b", a=free_shape[0])
        elif len(free_shape) == 3:
            v = v.rearrange("p (a b c) -> p a b c", a=free_shape[0], b=free_shape[1])
        return v


def _rope_tables(seq, ctx, dim):
    rows = seq // GRID_W
    r, c = np.meshgrid(np.arange(rows, dtype=np.float32), np.arange(GRID_W, dtype=np.float32), indexing="ij")
    half = dim // 2
    inv = (10000.0 ** (-np.arange(0, half, 2, dtype=np.float32) / half)).astype(np.float32)
    ar = r.reshape(-1, 1) * inv
    ac = c.reshape(-1, 1) * inv
    ang = np.concatenate([ar, ar, ac, ac], axis=-1).astype(np.float32)
    cos = np.concatenate([np.cos(ang), np.ones((ctx, dim), np.float32)], 0)
    sin = np.concatenate([np.sin(ang), np.zeros((ctx, dim), np.float32)], 0)
    return np.ascontiguousarray(cos.T.astype(np.float32)), np.ascontiguousarray(sin.T.astype(np.float32))


def _rot_matrix(dim):
    q = dim // 4
    R = np.zeros((dim, dim), np.float32)
    for dp in range(dim):
        blk = dp // q
        if blk in (0, 2):
            R[dp + q, dp] = -1.0
        else:
            R[dp - q, dp] = 1.0
    return R


def host_consts(seq, ctx):
    cosA, sinA = _rope_tables(seq, ctx, 64)
    cosB, sinB = _rope_tables(seq, ctx, 128)
    k = np.arange(128)
    U = (k[:, None] <= k[None, :]).astype(np.float32)
    L = (k[:, None] >= k[None, :]).astype(np.float32)
    nmf = np.where(k[:, None] <= k[None, :], 0.0, -30000.0).astype(np.float32)
    nmb = np.where(k[:, None] >= k[None, :], 0.0, -30000.0).astype(np.float32)
    return {
        "k_cosA": cosA, "k_sinA": sinA, "k_cosB": cosB, "k_sinB": sinB,
        "k_RA": _rot_matrix(64), "k_RB": _rot_matrix(128),
        "k_ident": np.eye(128, dtype=np.float32), "k_U": U, "k_L": L,
        "k_nmf": np.tile(nmf, (1, 4)), "k_nmb": np.tile(nmb, (1, 4)),
    }


def build_program(SEQ, CTX, depth=DEPTH, debug=False, stop=None):
    T = SEQ + CTX
    NT = T // 128
    NTL = SEQ // 128
    NTC = CTX // 128
    blocks = [(i * 512, 512) for i in range(SEQ // 512)] + [(SEQ, CTX)]
    nc = bass.Bass("TRN2", target_bir_lowering=False)

    def din(name, shape, dt=F32):
        return nc.dram_tensor(name, list(shape), dt, kind="ExternalInput").ap()

    def dscr(name, shape, dt):
        return nc.dram_tensor(name, list(shape), dt, kind=("ExternalOutput" if debug else "Internal")).ap()

    xin = din("xin", [T, D])
    c2 = din("c2", [2, D])
    W = {}
    for nm, shp in (("w_mod", [depth, D, 3 * D]), ("b_mod", [depth, 3 * D]), ("w_in", [depth, D, INW]),
                    ("mla_q_norm", [depth, 512]), ("mla_w_uq", [depth, 512, 1536]), ("mla_kv_norm", [depth, 256]),
                    ("mla_w_ukv", [depth, 256, 2048]), ("gqa_q_norm", [depth, 128]), ("gqa_k_norm", [depth, 128]),
                    ("ssd_conv_w", [depth, 5, 3072]), ("ssd_conv_b", [depth, 3072]), ("ssd_a_log", [depth, 64]),
                    ("ssd_dt_bias", [depth, 64]), ("ssd_d", [depth, 32]), ("ssd_norm", [depth, D]),
                    ("w_br", [depth, 4096, D]), ("w_out", [depth, D, D]), ("ln_g", [depth, D]), ("ln_b", [depth, D])):
        W[nm] = din(nm, shp)
    K = {}
    for nm, shp in (("k_cosA", [64, T]), ("k_sinA", [64, T]), ("k_cosB", [128, T]), ("k_sinB", [128, T]),
                    ("k_RA", [64, 64]), ("k_RB", [128, 128]), ("k_ident", [128, 128]), ("k_U", [128, 128]),
                    ("k_L", [128, 128]), ("k_nmf", [128, 512]), ("k_nmb", [128, 512])):
        K[nm] = din(nm, shp)
    out = nc.dram_tensor("out", [SEQ, D], F32, kind="ExternalOutput").ap()

    XR = dscr("XR", [T, D], F32)
    PJ = dscr("PJ", [INW, T], BF16)
    ZT = dscr("ZT", [T, 2048], BF16)
    GV = dscr("GV", [T, 256], BF16)
    DTR = dscr("DTR", [T, 64], F32)
    QA = dscr("QA", [8, 192, T], BF16)
    KAN = dscr("KAN", [8, 128, T], BF16)
    KPE = dscr("KPE", [64, T], BF16)
    VA = dscr("VA", [T, 1024], BF16)
    QB = dscr("QB", [8, 128, T], BF16)
    KB = dscr("KB", [2, 128, T], BF16)
    YG = dscr("YG", [4096, T], BF16)
    XC = dscr("XC", [T, 3072], BF16)
    XBT = dscr("XBT", [1024, T], BF16)
    YF = dscr("YF", [T, 2048], F32)
    WB16 = nc.dram_tensor("WB16", [16, 128, 32 * 128], BF16, kind="Internal").ap()
    WO16 = nc.dram_tensor("WO16", [4, 128, 16 * 512], BF16, kind="Internal").ap()

    es = ExitStack()
    S = Sched(nc)
    ARENA_BYTES = 206 * 1024
    arena_t = es.enter_context(nc.sbuf_tensor("arena", [128, ARENA_BYTES], U8))
    AR = Arena(arena_t, ARENA_BYTES - 8 * 1024)
    CAR = Arena(arena_t[:, ARENA_BYTES - 8 * 1024:ARENA_BYTES], 8 * 1024)
    PS = [es.enter_context(nc.psum_tensor("ps%d" % i, [128, 512], F32)) for i in range(8)]

    uid = [0]

    def key(p):
        uid[0] += 1
        return (p, uid[0])

    def dma(eng, out_, in_, reads=(), writes=(), semkey=None, slow=False):
        if slow:
            f = lambda e: e.dma_start(out=out_, in_=in_, allow_slow_non_contiguous=True)
        else:
            f = lambda e: e.dma_start(out=out_, in_=in_)
        return S.op(eng, f, reads=reads, writes=writes, dma=True, semkey=semkey)

    ident = CAR.alloc([128], F32)
    identb = CAR.alloc([128], BF16)
    onesb = CAR.alloc([128], BF16)
    onesf = CAR.alloc([128], F32)
    RAb = CAR.alloc([64], BF16)
    RBb = CAR.alloc([128], BF16)
    Uf = CAR.alloc([128], F32)
    Lf = CAR.alloc([128], F32)
    nmf = CAR.alloc([512], F32)
    nmb = CAR.alloc([512], F32)
    dma("sp", ident, K["k_ident"], writes=["ident"], semkey="c0")
    dma("sp", Uf, K["k_U"], writes=["Uf"], semkey="c1")
    dma("sp", Lf, K["k_L"], writes=["Lf"], semkey="c2")
    dma("sp", nmf, K["k_nmf"], writes=["nmf"], semkey="c3")
    dma("sp", nmb, K["k_nmb"], writes=["nmb"], semkey="c4")
    dma("pool", identb, K["k_ident"], writes=["identb"], semkey="c5")
    dma("pool", RAb[0:64, :], K["k_RA"], writes=["RAb"], semkey="c6")
    dma("pool", RBb, K["k_RB"], writes=["RBb"], semkey="c7")
    S.op("dve", lambda e: e.memset(onesb, 1.0), writes=["onesb"])
    S.op("dve", lambda e: e.memset(onesf, 1.0), writes=["onesf"])
    CONST_R = ["ident", "identb", "onesb", "onesf", "RAb", "RBb", "Uf", "Lf", "nmf", "nmb"]

    def const_barrier():
        pass

    for l in range(depth):
        last = (l == depth - 1)
        ctx_out = not last
        xsrc = xin if l == 0 else XR
        xdst = out if last else XR

        S.barrier()
        AR.reset()
        c2T = AR.alloc([16, 2], F32)
        scT = AR.alloc([16, 2], F32)
        bmT = AR.alloc([48], F32)
        shT = AR.alloc([16, 2], F32)
        s1T = AR.alloc([16, 2], F32)
        gbc = AR.alloc([2, 2048], F32)
        A_KEEP = AR.off
        bgate = AR.alloc([2048], F32)
        scbc = AR.alloc([2, 16, 128], F32)
        for r_ in range(2):
            dma("sp", c2T[:, :, r_], c2[r_].rearrange("(kc p) -> p kc", p=128), writes=["c2T"], semkey=("a0", r_), slow=True)
        dma("sp", bmT, W["b_mod"][l].rearrange("(t p) -> p t", p=128), writes=["bmT"], semkey="a1", slow=True)
        dma("sp", bgate, W["b_mod"][l:l + 1, 2 * D:3 * D].partition_broadcast(128), writes=["bgate"], semkey="a2")
        S.op("act", lambda e: e.activation(out=scT, in_=c2T, func=AF.Silu), reads=["c2T"], writes=["scT"])
        for r in range(2):
            for kc in range(16):
                S.op("dve", lambda e, r=r, kc=kc: e.tensor_copy(out=scbc[:, r, kc, :], in_=scT[:, kc, r:r + 1].to_broadcast([128, 128])),
                     reads=["scT"], writes=["scbc"])
        wmb = [AR.alloc([16, 512], F32) for _ in range(2)]
        for cb in range(12):
            wb = wmb[cb % 2]
            wk = ("wmb", cb % 2)
            dma("sp" if cb % 2 == 0 else "act", wb, W["w_mod"][l, :, cb * 512:(cb + 1) * 512].rearrange("(kc p) c -> p kc c", p=128),
                writes=[wk], semkey=wk)
            if cb < 8:
                pst = PS[cb % 2]
                for ct in range(4):
                    for kc in range(16):
                        S.op("pe", lambda e, wb=wb, ct=ct, kc=kc, pst=pst: e.matmul(pst[:, ct * 2:ct * 2 + 2], lhsT=wb[:, kc, ct * 128:(ct + 1) * 128], rhs=scT[:, kc, :], start=(kc == 0), stop=(kc == 15)),
                             reads=[wk, "scT"], writes=[("ps", cb % 2)])
                for ct in range(4):
                    gt = cb * 4 + ct
                    dst = shT if gt < 16 else s1T
                    kk = gt % 16
                    if gt < 16:
                        S.op("dve", lambda e, pst=pst, ct=ct, dst=dst, kk=kk, gt=gt: e.tensor_scalar_add(out=dst[:, kk, :], in0=pst[:, ct * 2:ct * 2 + 2], scalar1=bmT[:, gt:gt + 1]),
                             reads=[("ps", cb % 2), "bmT"], writes=["modT"])
                    else:
                        S.op("dve", lambda e, pst=pst, ct=ct, dst=dst, kk=kk, gt=gt: e.tensor_scalar(out=dst[:, kk, :], in0=pst[:, ct * 2:ct * 2 + 2], scalar1=bmT[:, gt:gt + 1], scalar2=1.0, op0=ALU.add, op1=ALU.add),
                             reads=[("ps", cb % 2), "bmT"], writes=["modT"])
            else:
                gcb = cb - 8
                for r in range(2):
                    pst = PS[2 + r]
                    for kc in range(16):
                        S.op("pe", lambda e, wb=wb, kc=kc, pst=pst, r=r: e.matmul(pst[:, :], lhsT=scbc[:, r, kc, :], rhs=wb[:, kc, :], start=(kc == 0), stop=(kc == 15)),
                             reads=[wk, "scbc"], writes=[("ps", 2 + r)])
                    S.op("dve", lambda e, pst=pst, r=r, gcb=gcb: e.tensor_tensor(out=gbc[:, r, gcb * 512:(gcb + 1) * 512], in0=pst[:, :], in1=bgate[:, gcb * 512:(gcb + 1) * 512], op=ALU.add),
                         reads=[("ps", 2 + r), "bgate"], writes=["gbc"])

        if stop == "A":
            break
        S.barrier()
        AR.off = A_KEEP
        xmodT = AR.alloc([16, T], BF16)
        B_KEEP = AR.off
        xt = [AR.alloc([2048], F32) for _ in range(2)]
        stats = [AR.alloc([4, 6], F32) for _ in range(2)]
        mv = [AR.alloc([2], F32) for _ in range(2)]
        rstd = [AR.alloc([1], F32) for _ in range(2)]
        for t in range(NT):
            b = t % 2
            r = 0 if t < NTL else 1
            xk = ("xt", b)
            dma("sp", xt[b], xsrc[t * 128:(t + 1) * 128, :], writes=[xk], semkey=xk)
            for c in range(4):
                S.op("dve", lambda e, b=b, c=c: e.bn_stats(out=stats[b][:, c, :], in_=xt[b][:, c * 512:(c + 1) * 512]), reads=[xk], writes=[("st", b)])
            S.op("dve", lambda e, b=b: e.bn_aggr(out=mv[b], in_=stats[b]), reads=[("st", b)], writes=[("mv", b)])
            S.op("dve", lambda e, b=b: e.tensor_scalar_add(out=rstd[b], in0=mv[b][:, 1:2], scalar1=EPS), reads=[("mv", b)], writes=[("rs", b)])
            S.op("act", lambda e, b=b: e.activation(out=rstd[b], in_=rstd[b], func=AF.Sqrt), reads=[("rs", b)], writes=[("rs", b)])
            S.op("dve", lambda e, b=b: e.reciprocal(out=rstd[b], in_=rstd[b]), reads=[("rs", b)], writes=[("rs", b)])
            S.op("dve", lambda e, b=b: e.tensor_scalar(out=xt[b], in0=xt[b], scalar1=mv[b][:, 0:1], scalar2=rstd[b][:, 0:1], op0=ALU.subtract, op1=ALU.mult),
                 reads=[xk, ("mv", b), ("rs", b)], writes=[xk])
            for g in range(4):
                pb = (t * 4 + g) % 4
                pst = PS[pb]
                for j in range(4):
                    kc = g * 4 + j
                    S.op("pe", lambda e, b=b, kc=kc, j=j, pst=pst: e.transpose(out=pst[:, j * 128:(j + 1) * 128], in_=xt[b][:, kc * 128:(kc + 1) * 128], identity=ident),
                         reads=[xk], writes=[("ps", pb)])
                for j in range(4):
                    kc = g * 4 + j
                    if j % 2 == 0:
                        S.op("act", lambda e, kc=kc, j=j, pst=pst, t=t, r=r: e.activation(out=xmodT[:, kc, t * 128:(t + 1) * 128], in_=pst[:, j * 128:(j + 1) * 128], func=AF.Identity, bias=shT[:, kc, r:r + 1], scale=s1T[:, kc, r:r + 1]),
                             reads=[("ps", pb)], writes=[("xm", t)])
                    else:
                        S.op("dve", lambda e, kc=kc, j=j, pst=pst, t=t, r=r: e.tensor_scalar(out=xmodT[:, kc, t * 128:(t + 1) * 128], in0=pst[:, j * 128:(j + 1) * 128], scalar1=s1T[:, kc, r:r + 1], scalar2=shT[:, kc, r:r + 1], op0=ALU.mult, op1=ALU.add),
                             reads=[("ps", pb)], writes=[("xm", t)])

        if stop == "B":
            break
        S.barrier()
        AR.off = B_KEEP
        wfm = [AR.alloc([16, 128], BF16) for _ in range(3)]
        ofm = [AR.alloc([512], BF16) for _ in range(4)]
        wtm = [AR.alloc([16, 256], BF16) for _ in range(2)]
        otm = [AR.alloc([256], BF16) for _ in range(2)]
        otf = [AR.alloc([64], F32) for _ in range(2)]
        win = W["w_in"][l]
        fm_tiles = []
        for (o0, n) in ((O_CQ, 512), (O_CKV, 256), (O_KR, 64), (O_GA, 1024), (O_GQ, 1024), (O_GK, 256), (O_GB, 1024), (O_XBC, 3072), (O_MG, 6144)):
            for c0 in range(o0, o0 + n, 128):
                fm_tiles.append((c0, min(128, o0 + n - c0)))
        ev = 0
        for i, (c0, ncol) in enumerate(fm_tiles):
            wb = wfm[i % 3]
            wk = ("wfm", i % 3)
            dma("pool", wb[:, :, 0:ncol], win[:, c0:c0 + ncol].rearrange("(kc p) c -> p kc c", p=128), writes=[wk], semkey=wk)
            for bi, (t0, n) in enumerate(blocks):
                pb = ev % 4
                pst = PS[pb]
                for kc in range(16):
                    S.op("pe", lambda e, wb=wb, kc=kc, pst=pst, t0=t0, n=n, ncol=ncol: e.matmul(pst[0:ncol, 0:n], lhsT=wb[:, kc, 0:ncol], rhs=xmodT[:, kc, t0:t0 + n], start=(kc == 0), stop=(kc == 15)),
                         reads=[wk], writes=[("ps", pb)])
                ob = ofm[ev % 4]
                ok_ = ("ofm", ev % 4)
                if ev % 2 == 0:
                    S.op("act", lambda e, ob=ob, pst=pst, n=n, ncol=ncol: e.activation(out=ob[0:ncol, 0:n], in_=pst[0:ncol, 0:n], func=AF.Copy), reads=[("ps", pb)], writes=[ok_])
                else:
                    S.op("dve", lambda e, ob=ob, pst=pst, n=n, ncol=ncol: e.tensor_copy(out=ob[0:ncol, 0:n], in_=pst[0:ncol, 0:n]), reads=[("ps", pb)], writes=[ok_])
                dma("sp", PJ[c0:c0 + ncol, t0:t0 + n], ob[0:ncol, 0:n], reads=[ok_], writes=[("PJ", c0, bi)], semkey=ok_)
                ev += 1
        tm_tiles = [(O_Z + i * 256, 256, "z", i * 256) for i in range(8)] + [(O_GV, 256, "gv", 0), (O_DT, 64, "dt", 0)]
        for i, (c0, ncol, kind, d0) in enumerate(tm_tiles):
            wb = wtm[i % 2]
            wk = ("wtm", i % 2)
            dma("pool", wb[:, :, 0:ncol], win[:, c0:c0 + ncol].rearrange("(kc p) c -> p kc c", p=128), writes=[wk], semkey=wk)
            for t in range(NT):
                pb = 4 + (ev % 4)
                pst = PS[pb]
                for kc in range(16):
                    S.op("pe", lambda e, wb=wb, kc=kc, pst=pst, t=t, ncol=ncol: e.matmul(pst[:, 0:ncol], lhsT=xmodT[:, kc, t * 128:(t + 1) * 128], rhs=wb[:, kc, 0:ncol], start=(kc == 0), stop=(kc == 15)),
                         reads=[wk], writes=[("ps", pb)])
                if kind == "dt":
                    ob = otf[ev % 2]
                    ok_ = ("otf", ev % 2)
                    dst = DTR[t * 128:(t + 1) * 128, :]
                else:
                    ob = otm[ev % 2]
                    ok_ = ("otm", ev % 2)
                    dst = (ZT if kind == "z" else GV)[t * 128:(t + 1) * 128, d0:d0 + ncol]
                if ev % 2 == 0:
                    S.op("act", lambda e, ob=ob, pst=pst, ncol=ncol: e.activation(out=ob[:, 0:ncol], in_=pst[:, 0:ncol], func=AF.Copy), reads=[("ps", pb)], writes=[ok_])
                else:
                    S.op("dve", lambda e, ob=ob, pst=pst, ncol=ncol: e.tensor_copy(out=ob[:, 0:ncol], in_=pst[:, 0:ncol]), reads=[("ps", pb)], writes=[ok_])
                dma("sp", dst, ob[:, 0:ncol], reads=[ok_], writes=[(kind, t, d0)], semkey=ok_)
                ev += 1

        if stop == "C":
            break
        phase_D(S, AR, A_KEEP, PS, l, W, K, PJ, QA, KAN, KPE, VA, QB, KB, blocks, T, NT, dma,
                dict(onesb=onesb, RAb=RAb, RBb=RBb))
        if stop == "D":
            break
        phase_F(S, AR, A_KEEP, PS, l, PJ, QA, KAN, KPE, VA, QB, KB, GV, YG, blocks, T, NT, NTL, NTC, SEQ, CTX, ctx_out, dma,
                dict(onesb=onesb, onesf=onesf))
        if stop == "F":
            break
        phase_G(S, AR, A_KEEP, PS, l, W, K, PJ, ZT, DTR, XC, XBT, YF, YG, T, NT, NTL, NTC, SEQ, CTX, ctx_out, dma,
                dict(onesb=onesb, onesf=onesf, identb=identb, ident=ident, Uf=Uf, Lf=Lf, nmf=nmf, nmb=nmb))
        if stop == "G":
            break
        phase_H(S, AR, A_KEEP, PS, l, W, PJ, YG, xsrc, xdst, gbc, blocks, T, NT, NTL, SEQ, ctx_out, last, dma, WB16, WO16)
        if stop == "H":
            break

    S.barrier()

    esem = {(e, k_): es.enter_context(nc.semaphore("s_%s%d" % (e, k_))) for e in ENGS for k_ in range(NSEM_ROT)}
    dsem = [es.enter_context(nc.semaphore("d%d" % i)) for i in range(40)]
    run = S.emit(esem, dsem)
    block = es.enter_context(nc.Block())

    @block.tensor
    def _(e):
        run("pe", e)

    @block.scalar
    def _(e):
        run("act", e)

    @block.vector
    def _(e):
        run("dve", e)

    @block.gpsimd
    def _(e):
        run("pool", e)

    @block.sync
    def _(e):
        run("sp", e)

    es.close()
    global LAST_S
    LAST_S = S
    return nc


def phase_D(S, AR, A_KEEP, PS, l, W, K, PJ, QA, KAN, KPE, VA, QB, KB, blocks, T, NT, dma, C):
    onesb, RAb, RBb = C["onesb"], C["RAb"], C["RBb"]
    S.barrier()
    AR.off = A_KEEP
    wuq = AR.alloc([4, 1536], BF16)
    wukv = AR.alloc([2, 2048], BF16)
    gq_ = AR.alloc([4], F32)
    gkv = AR.alloc([2], F32)
    gqb = AR.alloc([1], F32)
    gkb = AR.alloc([1], F32)
    dma("pool", wuq, W["mla_w_uq"][l].rearrange("(kc p) c -> p kc c", p=128), writes=["wuq"], semkey="d0")
    dma("pool", wukv, W["mla_w_ukv"][l].rearrange("(kc p) c -> p kc c", p=128), writes=["wukv"], semkey="d1")
    dma("sp", gq_, W["mla_q_norm"][l].rearrange("(t p) -> p t", p=128), writes=["gq_"], semkey="d2", slow=True)
    dma("sp", gkv, W["mla_kv_norm"][l].rearrange("(t p) -> p t", p=128), writes=["gkv"], semkey="d3", slow=True)
    dma("sp", gqb, W["gqa_q_norm"][l].rearrange("(t p) -> p t", p=128), writes=["gqb"], semkey="d4", slow=True)
    dma("sp", gkb, W["gqa_k_norm"][l].rearrange("(t p) -> p t", p=128), writes=["gkb"], semkey="d5", slow=True)
    wukv4 = wukv.rearrange("p k (h c) -> p k h c", h=8)
    cosA = AR.alloc([512], F32)
    sinA = AR.alloc([512], F32)
    cosB = AR.alloc([512], F32)
    sinB = AR.alloc([512], F32)
    xin_ = AR.alloc([4, 512], BF16)
    sq = AR.alloc([4, 512], BF16)
    xn = AR.alloc([4, 512], BF16)
    r1 = AR.alloc([512], F32)
    gin = [AR.alloc([512], BF16) for _ in range(2)]
    gsq = [AR.alloc([512], BF16) for _ in range(2)]
    gr = [AR.alloc([512], F32) for _ in range(2)]
    qf = [AR.alloc([512], F32) for _ in range(2)]
    qb_ = [AR.alloc([512], BF16) for _ in range(2)]
    t1 = [AR.alloc([512], F32) for _ in range(2)]
    t2 = [AR.alloc([512], F32) for _ in range(2)]
    ob = [AR.alloc([512], BF16) for _ in range(4)]
    cnt = {"ob": 0, "ps": 0, "g": 0}

    def next_ob():
        i = cnt["ob"] % 4
        cnt["ob"] += 1
        return ob[i], ("ob", i)

    def next_ps():
        i = cnt["ps"] % 6
        cnt["ps"] += 1
        return PS[i], ("ps", i)

    def store(dst, src_ps, np_, n, wkey, eng):
        o_, ok_ = next_ob()
        if eng == "act":
            S.op("act", lambda e: e.activation(out=o_[0:np_, 0:n], in_=src_ps, func=AF.Copy), reads=[wkey], writes=[ok_])
        else:
            S.op("dve", lambda e: e.tensor_copy(out=o_[0:np_, 0:n], in_=src_ps), reads=[wkey], writes=[ok_])
        kx = key_dram(dst)
        dma("sp", dst, o_[0:np_, 0:n], reads=[ok_], writes=[kx], semkey=ok_)
        return kx

    kd = [0]

    def key_dram(dst):
        kd[0] += 1
        return ("dram", kd[0])

    def rms_rows(src_row0, nk, gain, n, t0, inv_n):
        dma("sp", xin_[:, 0:nk, 0:n], PJ[src_row0:src_row0 + nk * 128, t0:t0 + n].rearrange("(kt p) n -> p kt n", p=128),
            writes=["xin_"], semkey="xin_")
        S.op("act", lambda e: e.activation(out=sq[:, 0:nk, 0:n], in_=xin_[:, 0:nk, 0:n], func=AF.Square), reads=["xin_"], writes=["sq"])
        pst, pk = next_ps()
        for kt in range(nk):
            S.op("pe", lambda e, kt=kt: e.matmul(pst[:, 0:n], lhsT=onesb, rhs=sq[:, kt, 0:n], start=(kt == 0), stop=(kt == nk - 1)), reads=["sq"], writes=[pk])
        S.op("dve", lambda e: e.tensor_scalar(out=r1[:, 0:n], in0=pst[:, 0:n], scalar1=inv_n, scalar2=EPS, op0=ALU.mult, op1=ALU.add), reads=[pk], writes=["r1"])
        S.op("act", lambda e: e.activation(out=r1[:, 0:n], in_=r1[:, 0:n], func=AF.Sqrt), reads=["r1"], writes=["r1"])
        S.op("dve", lambda e: e.reciprocal(out=r1[:, 0:n], in_=r1[:, 0:n]), reads=["r1"], writes=["r1"])
        for kt in range(nk):
            S.op("dve", lambda e, kt=kt: e.scalar_tensor_tensor(out=xn[:, kt, 0:n], in0=xin_[:, kt, 0:n], scalar=gain[:, kt:kt + 1], in1=r1[:, 0:n], op0=ALU.mult, op1=ALU.mult),
                 reads=["xin_", "r1"], writes=["xn"])

    def rope(src_ps, pk, np_, n, Rm, cos_, sin_, dst):
        i = cnt["g"] % 2
        cnt["g"] += 1
        S.op("act", lambda e: e.activation(out=qf[i][0:np_, 0:n], in_=src_ps, func=AF.Copy), reads=[pk], writes=[("qf", i)])
        S.op("dve", lambda e: e.tensor_copy(out=qb_[i][0:np_, 0:n], in_=src_ps), reads=[pk], writes=[("qb", i)])
        rope_sb(i, np_, n, Rm, cos_, sin_, dst)

    def rope_sb(i, np_, n, Rm, cos_, sin_, dst, extra_reads=()):
        pst2, pk2 = next_ps()
        S.op("pe", lambda e: e.matmul(pst2[0:np_, 0:n], lhsT=Rm[0:np_, 0:np_], rhs=qb_[i][0:np_, 0:n], start=True, stop=True), reads=[("qb", i)], writes=[pk2])
        S.op("dve", lambda e: e.tensor_tensor(out=t1[i][0:np_, 0:n], in0=qf[i][0:np_, 0:n], in1=cos_[0:np_, 0:n], op=ALU.mult), reads=[("qf", i), "tab"], writes=[("t1", i)])
        S.op("dve", lambda e: e.tensor_tensor(out=t2[i][0:np_, 0:n], in0=pst2[0:np_, 0:n], in1=sin_[0:np_, 0:n], op=ALU.mult), reads=[pk2, "tab"], writes=[("t2", i)])
        o_, ok_ = next_ob()
        S.op("pool", lambda e: e.tensor_tensor(out=o_[0:np_, 0:n], in0=t1[i][0:np_, 0:n], in1=t2[i][0:np_, 0:n], op=ALU.add), reads=[("t1", i), ("t2", i)], writes=[ok_])
        dma("sp", dst, o_[0:np_, 0:n], reads=[ok_], writes=[key_dram(dst)] + list(extra_reads), semkey=ok_)

    def do_block(bi, t0, n):
        dma("sp", cosA[0:64, 0:n], K["k_cosA"][:, t0:t0 + n], writes=["tab"], semkey="tabA")
        dma("sp", sinA[0:64, 0:n], K["k_sinA"][:, t0:t0 + n], writes=["tab"], semkey="tabA2")
        dma("sp", cosB[:, 0:n], K["k_cosB"][:, t0:t0 + n], writes=["tab"], semkey="tabB")
        dma("sp", sinB[:, 0:n], K["k_sinB"][:, t0:t0 + n], writes=["tab"], semkey="tabB2")
        rms_rows(O_CQ, 4, gq_, n, t0, 1.0 / 512)
        if DSTOP == 11:
            return
        for h in range(8):
            pst, pk = next_ps()
            for kc in range(4):
                S.op("pe", lambda e, kc=kc, h=h, pst=pst: e.matmul(pst[:, 0:n], lhsT=wuq[:, kc, h * 192:h * 192 + 128], rhs=xn[:, kc, 0:n], start=(kc == 0), stop=(kc == 3)),
                     reads=["wuq", "xn"], writes=[pk])
            store(QA[h, 0:128, t0:t0 + n], pst[:, 0:n], 128, n, pk, "act" if h % 2 else "dve")
            if DSTOP == 12:
                continue
            pst, pk = next_ps()
            for kc in range(4):
                S.op("pe", lambda e, kc=kc, h=h, pst=pst: e.matmul(pst[0:64, 0:n], lhsT=wuq[:, kc, h * 192 + 128:h * 192 + 192], rhs=xn[:, kc, 0:n], start=(kc == 0), stop=(kc == 3)),
                     reads=["wuq", "xn"], writes=[pk])
            kx = store(QA[h, 128:192, t0:t0 + n], pst[0:64, 0:n], 64, n, pk, "dve" if h % 2 else "act")
            i = cnt["g"] % 2
            cnt["g"] += 1
            dma("sp", qb_[i][0:64, 0:n], QA[h, 128:192, t0:t0 + n], reads=[kx], writes=[("qb", i)], semkey=("qbl", i))
            S.op("act", lambda e, i=i: e.activation(out=qf[i][0:64, 0:n], in_=qb_[i][0:64, 0:n], func=AF.Copy), reads=[("qb", i)], writes=[("qf", i)])
            rope_sb(i, 64, n, RAb, cosA, sinA, QA[h, 128:192, t0:t0 + n], extra_reads=[kx])
        if DSTOP == 1:
            return
        rms_rows(O_CKV, 2, gkv, n, t0, 1.0 / 256)
        for h in range(8):
            pst, pk = next_ps()
            for kc in range(2):
                S.op("pe", lambda e, kc=kc, h=h, pst=pst: e.matmul(pst[:, 0:n], lhsT=wukv[:, kc, h * 256:h * 256 + 128], rhs=xn[:, kc, 0:n], start=(kc == 0), stop=(kc == 1)),
                     reads=["wukv", "xn"], writes=[pk])
            store(KAN[h, :, t0:t0 + n], pst[:, 0:n], 128, n, pk, "act" if h % 2 else "dve")
        for tt in range(n // 128):
            for half in range(2):
                pst, pk = next_ps()
                for hd in range(4):
                    for kc in range(2):
                        c0_ = (half * 4 + hd) * 256 + 128
                        S.op("pe", lambda e, kc=kc, hd=hd, c0_=c0_, tt=tt, pst=pst: e.matmul(pst[:, hd * 128:(hd + 1) * 128], lhsT=xn[:, kc, tt * 128:(tt + 1) * 128],
                                                                                     rhs=wukv[:, kc, c0_:c0_ + 128], start=(kc == 0), stop=(kc == 1)),
                             reads=["wukv", "xn"], writes=[pk])
                store(VA[t0 + tt * 128:t0 + (tt + 1) * 128, half * 512:(half + 1) * 512], pst[:, :], 128, 512, pk, "act" if half else "dve")
        if DSTOP == 2:
            return
        i = cnt["g"] % 2
        cnt["g"] += 1
        dma("sp", qb_[i][0:64, 0:n], PJ[O_KR:O_KR + 64, t0:t0 + n], writes=[("qb", i)], semkey=("qbl", i))
        S.op("act", lambda e, i=i: e.activation(out=qf[i][0:64, 0:n], in_=qb_[i][0:64, 0:n], func=AF.Copy), reads=[("qb", i)], writes=[("qf", i)])
        rope_sb(i, 64, n, RAb, cosA, sinA, KPE[:, t0:t0 + n])
        if DSTOP == 3:
            return
        for hh in range(10):
            row0 = O_GQ + hh * 128 if hh < 8 else O_GK + (hh - 8) * 128
            gain = gqb if hh < 8 else gkb
            dst = QB[hh, :, t0:t0 + n] if hh < 8 else KB[hh - 8, :, t0:t0 + n]
            j = hh % 2
            dma("sp", gin[j][:, 0:n], PJ[row0:row0 + 128, t0:t0 + n], writes=[("gin", j)], semkey=("gin", j))
            S.op("act", lambda e, j=j: e.activation(out=gsq[j][:, 0:n], in_=gin[j][:, 0:n], func=AF.Square), reads=[("gin", j)], writes=[("gsq", j)])
            pst, pk = next_ps()
            S.op("pe", lambda e, j=j, pst=pst: e.matmul(pst[:, 0:n], lhsT=onesb, rhs=gsq[j][:, 0:n], start=True, stop=True), reads=[("gsq", j)], writes=[pk])
            S.op("dve", lambda e, j=j, pst=pst: e.tensor_scalar(out=gr[j][:, 0:n], in0=pst[:, 0:n], scalar1=1.0 / 128, scalar2=EPS, op0=ALU.mult, op1=ALU.add), reads=[pk], writes=[("gr", j)])
            S.op("act", lambda e, j=j: e.activation(out=gr[j][:, 0:n], in_=gr[j][:, 0:n], func=AF.Sqrt), reads=[("gr", j)], writes=[("gr", j)])
            S.op("dve", lambda e, j=j: e.reciprocal(out=gr[j][:, 0:n], in_=gr[j][:, 0:n]), reads=[("gr", j)], writes=[("gr", j)])
            i = cnt["g"] % 2
            cnt["g"] += 1
            S.op("dve", lambda e, j=j, i=i, gain=gain: e.scalar_tensor_tensor(out=qf[i][:, 0:n], in0=gin[j][:, 0:n], scalar=gain[:, 0:1], in1=gr[j][:, 0:n], op0=ALU.mult, op1=ALU.mult),
                 reads=[("gin", j), ("gr", j)], writes=[("qf", i)])
            S.op("act", lambda e, i=i: e.activation(out=qb_[i][:, 0:n], in_=qf[i][:, 0:n], func=AF.Copy), reads=[("qf", i)], writes=[("qb", i)])
            rope_sb(i, 128, n, RBb, cosB, sinB, dst)

    for bi, (t0, n) in enumerate(blocks):
        do_block(bi, t0, n)


def phase_F(S, AR, A_KEEP, PS, l, PJ, QA, KAN, KPE, VA, QB, KB, GV, YG, blocks, T, NT, NTL, NTC, SEQ, CTX, ctx_out, dma, C):
    onesb = C["onesb"]
    S.barrier()
    AR.off = A_KEEP
    kpe = AR.alloc([T], BF16)
    kn = [AR.alloc([T], BF16) for _ in range(2)]
    vs = [AR.alloc([NT, 128], BF16) for _ in range(2)]
    qn = [AR.alloc([512], BF16) for _ in range(2)]
    qp = [AR.alloc([512], BF16) for _ in range(2)]
    pt = [AR.alloc([512], BF16) for _ in range(3)]
    gl = [AR.alloc([512], BF16) for _ in range(2)]
    gs = [AR.alloc([512], F32) for _ in range(2)]
    rinv = [AR.alloc([512], F32) for _ in range(2)]
    yo = [AR.alloc([512], F32) for _ in range(2)]
    yb = [AR.alloc([512], BF16) for _ in range(2)]
    lacc = [AR.alloc([512], F32) for _ in range(2)]
    onesf = C["onesf"]
    dma("sp", kpe[0:64, :], KPE[:, :], writes=["kpe"], semkey="kpe")
    it = [0]
    sidx = [0]

    def do_q(kind, kv, kb_, h, bi, t0, n, scale):
        is_ctx = (t0 >= SEQ)
        kts = list(range(NTL, NT)) if is_ctx else list(range(NT))
        j = it[0] % 2
        it[0] += 1
        if kind == "A":
            dma("sp", qn[j][:, 0:n], QA[h, 0:128, t0:t0 + n], writes=[("qn", j)], semkey=("qn", j))
            dma("sp", qp[j][0:64, 0:n], QA[h, 128:192, t0:t0 + n], writes=[("qp", j)], semkey=("qp", j))
            grow = O_GA + h * 128
        else:
            dma("sp", qn[j][:, 0:n], QB[h, :, t0:t0 + n], writes=[("qn", j)], semkey=("qn", j))
            grow = O_GB + h * 128
        dma("sp", gl[j][:, 0:n], PJ[grow:grow + 128, t0:t0 + n], writes=[("gl", j)], semkey=("gl", j))
        S.op("act", lambda e: e.activation(out=gs[j][:, 0:n], in_=gl[j][:, 0:n], func=AF.Silu), reads=[("gl", j)], writes=[("gs", j)])
        po = PS[3 + j]
        pl = PS[5 + j]
        pok = ("ps", 3 + j)
        plk = ("ps", 5 + j)

        def s_mm(kt):
            si = sidx[0] % 3
            sidx[0] += 1
            pst = PS[si]
            if kind == "A":
                S.op("pe", lambda e: e.matmul(pst[:, 0:n], lhsT=kn[kb_][:, kt * 128:(kt + 1) * 128], rhs=qn[j][:, 0:n], start=True, stop=False),
                     reads=[("kn", kb_), ("qn", j)], writes=[("ps", si)])
                S.op("pe", lambda e: e.matmul(pst[:, 0:n], lhsT=kpe[0:64, kt * 128:(kt + 1) * 128], rhs=qp[j][0:64, 0:n], start=False, stop=True),
                     reads=["kpe", ("qp", j)], writes=[("ps", si)])
            else:
                S.op("pe", lambda e: e.matmul(pst[:, 0:n], lhsT=kn[kb_][:, kt * 128:(kt + 1) * 128], rhs=qn[j][:, 0:n], start=True, stop=True),
                     reads=[("kn", kb_), ("qn", j)], writes=[("ps", si)])
            return si

        def pv(kt, si, first, lastk):
            pst = PS[si]
            p_ = pt[si]
            S.op("act", lambda e: e.activation(out=p_[:, 0:n], in_=pst[:, 0:n], func=AF.Exp, scale=scale), reads=[("ps", si)], writes=[("pt", si)])
            S.op("pe", lambda e: e.matmul(po[:, 0:n], lhsT=vs[kb_][:, kt, :], rhs=p_[:, 0:n], start=first, stop=lastk),
                 reads=[("vs", kb_), ("pt", si)], writes=[pok])
            if first:
                S.op("pool", lambda e: e.tensor_copy(out=lacc[j][:, 0:n], in_=p_[:, 0:n]), reads=[("pt", si)], writes=[("lacc", j)])
            else:
                S.op("pool", lambda e: e.tensor_tensor(out=lacc[j][:, 0:n], in0=lacc[j][:, 0:n], in1=p_[:, 0:n], op=ALU.add), reads=[("pt", si), ("lacc", j)], writes=[("lacc", j)])
            if lastk:
                S.op("pe", lambda e: e.matmul(pl[:, 0:n], lhsT=onesf, rhs=lacc[j][:, 0:n], start=True, stop=True), reads=[("lacc", j)], writes=[plk])

        pend = []
        for kt in kts:
            si = s_mm(kt)
            pend.append((kt, si))
            if len(pend) > 1:
                k0, s0 = pend.pop(0)
                pv(k0, s0, first=(k0 == kts[0]), lastk=False)
        k0, s0 = pend.pop(0)
        pv(k0, s0, first=(k0 == kts[0]), lastk=True)
        S.op("dve", lambda e: e.reciprocal(out=rinv[j][:, 0:n], in_=pl[:, 0:n]), reads=[plk], writes=[("rinv", j)])
        S.op("dve", lambda e: e.tensor_tensor(out=yo[j][:, 0:n], in0=po[:, 0:n], in1=rinv[j][:, 0:n], op=ALU.mult), reads=[pok, ("rinv", j)], writes=[("yo", j)])
        S.op("pool", lambda e: e.tensor_tensor(out=yb[j][:, 0:n], in0=yo[j][:, 0:n], in1=gs[j][:, 0:n], op=ALU.mult), reads=[("yo", j), ("gs", j)], writes=[("yb", j)])
        yrow = (0 if kind == "A" else 1024) + h * 128
        dma("sp", YG[yrow:yrow + 128, t0:t0 + n], yb[j][:, 0:n], reads=[("yb", j)], writes=[("YG", yrow, bi)], semkey=("yb", j))

    for kind in ("A", "B"):
        nkv = 8 if kind == "A" else 2
        grp = 1 if kind == "A" else 4
        scale = (192.0 if kind == "A" else 128.0) ** -0.5
        for kv in range(nkv):
            kb_ = kv % 2
            kk = ("kn", kb_)
            vk = ("vs", kb_)
            if kind == "A":
                dma("sp", kn[kb_], KAN[kv, :, :], writes=[kk], semkey=kk)
                for c0_ in range(0, NT, 8):
                    c1_ = min(NT, c0_ + 8)
                    dma("act", vs[kb_][:, c0_:c1_, :], VA[c0_ * 128:c1_ * 128, kv * 128:(kv + 1) * 128].rearrange("(t p) d -> p t d", p=128), writes=[vk], semkey=vk)
            else:
                dma("sp", kn[kb_], KB[kv, :, :], writes=[kk], semkey=kk)
                for c0_ in range(0, NT, 8):
                    c1_ = min(NT, c0_ + 8)
                    dma("act", vs[kb_][:, c0_:c1_, :], GV[c0_ * 128:c1_ * 128, kv * 128:(kv + 1) * 128].rearrange("(t p) d -> p t d", p=128), writes=[vk], semkey=vk)
            for g in range(grp):
                h = kv * grp + g
                for bi, (t0, n) in enumerate(blocks):
                    if t0 >= SEQ and not ctx_out:
                        continue
                    do_q(kind, kv, kb_, h, bi, t0, n, scale)


def phase_G(S, AR, A_KEEP, PS, l, W, K, PJ, ZT, DTR, XC, XBT, YF, YG, T, NT, NTL, NTC, SEQ, CTX, ctx_out, dma, C):
    onesf, identb, ident, Uf, Lf, nmf, nmb = C["onesf"], C["identb"], C["ident"], C["Uf"], C["Lf"], C["nmf"], C["nmb"]
    S.barrier()
    AR.off = A_KEEP
    cw = AR.alloc([24, 5], F32)
    cbias = AR.alloc([24], F32)
    for k_ in range(5):
        dma("sp", cw[:, :, k_], W["ssd_conv_w"][l, k_].rearrange("(t p) -> p t", p=128), writes=["cw"], semkey=("g0", k_), slow=True)
    dma("sp", cbias, W["ssd_conv_b"][l].rearrange("(t p) -> p t", p=128), writes=["cbias"], semkey="g1", slow=True)
    xi = [AR.alloc([T + 8], BF16) for _ in range(2)]
    acc = [AR.alloc([T], F32) for _ in range(2)]
    co = [AR.alloc([T], BF16) for _ in range(4)]
    to = [AR.alloc([512], BF16) for _ in range(2)]
    for b in range(2):
        S.op("pool", lambda e, b=b: e.memset(xi[b], 0.0), writes=[("xi", b)])
    segs = [(0, SEQ, 0), (SEQ, CTX, 4)]

    def conv_tile(ct):
        b = ct % 2
        row0 = O_XBC + ct * 128
        dma("sp", xi[b][:, 2:2 + SEQ], PJ[row0:row0 + 128, 0:SEQ], writes=[("xi", b)], semkey=("xi", b))
        dma("sp", xi[b][:, 6 + SEQ:6 + T], PJ[row0:row0 + 128, SEQ:T], writes=[("xi", b)], semkey=("xi2", b))
        for (t0, n, ex) in segs:
            S.op("dve", lambda e, t0=t0, n=n, ex=ex: e.tensor_scalar_mul(out=acc[b][:, t0:t0 + n], in0=xi[b][:, t0 + ex:t0 + ex + n], scalar1=cw[:, ct, 0:1]),
                 reads=[("xi", b), "cw"], writes=[("acc", b)])
            for k in range(1, 5):
                S.op("dve", lambda e, k=k, t0=t0, n=n, ex=ex: e.scalar_tensor_tensor(out=acc[b][:, t0:t0 + n], in0=xi[b][:, t0 + ex + k:t0 + ex + k + n], scalar=cw[:, ct, k:k + 1], in1=acc[b][:, t0:t0 + n], op0=ALU.mult, op1=ALU.add),
                     reads=[("xi", b), "cw"], writes=[("acc", b)])
        c = ct % 4
        S.op("act", lambda e: e.activation(out=co[c], in_=acc[b], func=AF.Silu, bias=cbias[:, ct:ct + 1]), reads=[("acc", b), "cbias"], writes=[("co", c)])
        if ct >= 16:
            dma("act", XBT[(ct - 16) * 128:(ct - 15) * 128, :], co[c], reads=[("co", c)], writes=[("XBT", ct)], semkey=("co", c))

    def transp_group(q, t, k):
        pb = k % 2
        pstb = PS[pb][:, :].bitcast(BF16)
        for j in range(4):
            S.op("pe", lambda e, j=j: e.transpose(out=pstb[:, j * 128:(j + 1) * 128], in_=co[j][:, t * 128:(t + 1) * 128], identity=identb),
                 reads=[("co", j)], writes=[("ps", pb)])
        tb = k % 2
        if k % 2 == 0:
            S.op("act", lambda e: e.activation(out=to[tb], in_=pstb[:, 0:512], func=AF.Copy), reads=[("ps", pb)], writes=[("to", tb)])
        else:
            S.op("dve", lambda e: e.tensor_copy(out=to[tb], in_=pstb[:, 0:512]), reads=[("ps", pb)], writes=[("to", tb)])
        dma("sp", XC[t * 128:(t + 1) * 128, q * 512:(q + 1) * 512], to[tb], reads=[("to", tb)], writes=[("XC", q, t)], semkey=("to", tb))

    kk = 0
    for q in range(6):
        for j in range(4):
            conv_tile(q * 4 + j)
        for t in range(NT):
            transp_group(q, t, kk)
            kk += 1

    if DSTOP == 21:
        return
    S.barrier()
    AR.off = A_KEEP
    abc = AR.alloc([64], F32)
    dtb = AR.alloc([64], F32)
    dsk = AR.alloc([32, 1], F32)
    gnorm = AR.alloc([2048], F32)
    hT = AR.alloc([4, 512], F32)
    hb = AR.alloc([4, 512], BF16)
    dma("sp", abc, W["ssd_a_log"][l:l + 1, :].partition_broadcast(128), writes=["abc"], semkey="g2")
    dma("sp", dtb, W["ssd_dt_bias"][l:l + 1, :].partition_broadcast(128), writes=["dtb"], semkey="g3")
    dma("sp", dsk[:, :, 0], W["ssd_d"][l:l + 1, :].partition_broadcast(128), writes=["dsk"], semkey="g4")
    dma("sp", gnorm, W["ssd_norm"][l:l + 1, :].partition_broadcast(128), writes=["gnorm"], semkey="g5")
    S.op("act", lambda e: e.activation(out=abc, in_=abc, func=AF.Exp), reads=["abc"], writes=["abc"])
    S.op("dve", lambda e: e.tensor_scalar_mul(out=abc, in0=abc, scalar1=-1.0), reads=["abc"], writes=["abc"])
    xc = [AR.alloc([3072], BF16) for _ in range(2)]
    dtr = [AR.alloc([64], F32) for _ in range(2)]
    bct = [AR.alloc([8, 128], BF16) for _ in range(2)]
    dt_ = AR.alloc([32, 1], F32)
    dta = AR.alloc([32], F32)
    Acol = AR.alloc([32, 1], F32)
    dte = AR.alloc([32, 1], F32)
    eA = AR.alloc([32, 1], F32)
    dec = AR.alloc([32, 1], F32)
    tmp32 = AR.alloc([32], F32)
    atot = AR.alloc([32], F32)
    xdtf = AR.alloc([32, 64], F32)
    xdtb = AR.alloc([32, 64], BF16)
    xdte = AR.alloc([32, 64], BF16)
    cbT = AR.alloc([4, 128], F32)
    seg = [AR.alloc([4, 128], F32) for _ in range(2)]
    Em = [AR.alloc([4, 128], F32) for _ in range(2)]
    MT = [AR.alloc([4, 128], BF16) for _ in range(2)]
    yoff = AR.alloc([8, 64], F32)
    ych = [AR.alloc([2048], F32) for _ in range(2)]
    yfl = AR.alloc([2048], F32)
    zt = AR.alloc([2048], BF16)
    zs = AR.alloc([2048], F32)
    junk = AR.alloc([512], F32)
    ss = AR.alloc([4], F32)
    vn = AR.alloc([2048], BF16)
    ygs = [AR.alloc([4, 128], BF16) for _ in range(2)]
    cnt = {"c": 0, "s": 0, "y": 0}

    def chunk(tt, d, want_y):
        b = cnt["c"] % 2
        cnt["c"] += 1
        Tri = Uf if d == 0 else Lf
        nm = nmf if d == 0 else nmb
        dma("sp", xc[b], XC[tt * 128:(tt + 1) * 128, :], writes=[("xc", b)], semkey=("xc", b))
        dma("sp", dtr[b], DTR[tt * 128:(tt + 1) * 128, :], writes=[("dtr", b)], semkey=("dtr", b))
        dma("act", bct[b], XBT[:, tt * 128:(tt + 1) * 128].rearrange("(r p) t -> p r t", p=128), writes=[("bct", b)], semkey=("bct", b))
        xs3 = xc[b][:, 0:2048].rearrange("p (h c) -> p h c", h=32)
        S.op("dve", lambda e: e.tensor_tensor(out=dt_[:, :, 0], in0=dtr[b][:, d * 32:(d + 1) * 32], in1=dtb[:, d * 32:(d + 1) * 32], op=ALU.add), reads=[("dtr", b), "dtb"], writes=["dt"])
        S.op("act", lambda e: e.activation(out=dt_, in_=dt_, func=AF.Exp), reads=["dt"], writes=["dt"])
        S.op("dve", lambda e: e.tensor_scalar_add(out=dt_, in0=dt_, scalar1=1.0), reads=["dt"], writes=["dt"])
        S.op("act", lambda e: e.activation(out=dt_, in_=dt_, func=AF.Ln), reads=["dt"], writes=["dt"])
        S.op("dve", lambda e: e.tensor_tensor(out=dta, in0=dt_[:, :, 0], in1=abc[:, d * 32:(d + 1) * 32], op=ALU.mult), reads=["dt", "abc"], writes=["dta"])
        pa = PS[0]
        S.op("pe", lambda e: e.matmul(pa[:, 0:32], lhsT=Tri, rhs=dta, start=True, stop=True), reads=["dta"], writes=[("ps", 0)])
        S.op("pe", lambda e: e.matmul(pa[:, 32:64], lhsT=onesf, rhs=dta, start=True, stop=True), reads=["dta"], writes=[("ps", 0)])
        S.op("act", lambda e: e.activation(out=Acol[:, :, 0], in_=pa[:, 0:32], func=AF.Copy), reads=[("ps", 0)], writes=["Acol"])
        S.op("act", lambda e: e.activation(out=atot, in_=pa[:, 32:64], func=AF.Copy), reads=[("ps", 0)], writes=["atot"])
        S.op("dve", lambda e: e.tensor_tensor(out=tmp32, in0=atot, in1=Acol[:, :, 0], op=ALU.subtract), reads=["atot", "Acol"], writes=["tmp32"])
        S.op("act", lambda e: e.activation(out=dte[:, :, 0], in_=tmp32, func=AF.Exp), reads=["tmp32"], writes=["dte"])
        S.op("act", lambda e: e.activation(out=eA[:, :, 0], in_=Acol[:, :, 0], func=AF.Exp), reads=["Acol"], writes=["eA"])
        S.op("act", lambda e: e.activation(out=dec[:, :, 0], in_=pa[:, 32:64], func=AF.Exp), reads=[("ps", 0)], writes=["dec"])
        S.op("dve", lambda e: e.tensor_tensor(out=xdtf, in0=xs3, in1=dt_.to_broadcast([128, 32, 64]), op=ALU.mult), reads=[("xc", b), "dt"], writes=["xdtf"])
        S.op("pool", lambda e: e.tensor_copy(out=xdtb, in_=xdtf), reads=["xdtf"], writes=["xdtb"])
        S.op("dve", lambda e: e.tensor_tensor(out=xdte, in0=xdtf, in1=dte.to_broadcast([128, 32, 64]), op=ALU.mult), reads=["xdtf", "dte"], writes=["xdte"])
        pcb = PS[1]
        for g in range(4):
            S.op("pe", lambda e, g=g: e.matmul(pcb[:, g * 128:(g + 1) * 128], lhsT=bct[b][:, g, :], rhs=bct[b][:, 4 + g, :], start=True, stop=True),
                 reads=[("bct", b)], writes=[("ps", 1)])
        S.op("act", lambda e: e.activation(out=cbT, in_=pcb[:, :].rearrange("p (g c) -> p g c", g=4), func=AF.Copy), reads=[("ps", 1)], writes=["cbT"])
        yb_ = cnt["y"] % 2
        cnt["y"] += 1
        y_ = ych[yb_]
        for g in range(4):
            py = PS[4 + (g % 2)]
            pyk = ("ps", 4 + (g % 2))
            if want_y:
                for half in range(2):
                    si = cnt["s"] % 2
                    cnt["s"] += 1
                    psg = PS[2 + si]
                    psk = ("ps", 2 + si)
                    h0 = g * 8 + half * 4
                    S.op("pe", lambda e, psg=psg: e.matmul(psg[:, :], lhsT=ident, rhs=nm, start=True, stop=False), writes=[psk])
                    for hh in range(4):
                        S.op("pe", lambda e, psg=psg, hh=hh, h0=h0: e.matmul(psg[:, hh * 128:(hh + 1) * 128], lhsT=dta[:, h0 + hh:h0 + hh + 1].to_broadcast([128, 128]), rhs=Tri, start=False, stop=(hh == 3)),
                             reads=["dta"], writes=[psk])
                    S.op("dve", lambda e, psg=psg, si=si, h0=h0: e.tensor_tensor(out=seg[si], in0=psg[:, :].rearrange("p (a c) -> p a c", a=4), in1=Acol[:, h0:h0 + 4, :].to_broadcast([128, 4, 128]), op=ALU.subtract),
                         reads=[psk, "Acol"], writes=[("seg", si)])
                    S.op("act", lambda e, si=si: e.activation(out=Em[si], in_=seg[si], func=AF.Exp), reads=[("seg", si)], writes=[("Em", si)])
                    S.op("dve", lambda e, si=si, g=g: e.tensor_tensor(out=MT[si], in0=Em[si], in1=cbT[:, g:g + 1, :].to_broadcast([128, 4, 128]), op=ALU.mult),
                         reads=[("Em", si), "cbT"], writes=[("MT", si)])
                    for hh in range(4):
                        e_ = half * 4 + hh
                        S.op("pe", lambda e, si=si, hh=hh, e_=e_, py=py, h0=h0: e.matmul(py[:, e_ * 64:(e_ + 1) * 64], lhsT=MT[si][:, hh, :], rhs=xdtb[:, h0 + hh, :], start=True, stop=True),
                             reads=[("MT", si), "xdtb"], writes=[pyk])
                po = PS[6]
                S.op("pe", lambda e, g=g, po=po: e.matmul(po[:, :], lhsT=bct[b][:, 4 + g, :], rhs=hb[:, g, :], start=True, stop=True), reads=[("bct", b), "hb"], writes=[("ps", 6)])
                S.op("dve", lambda e, g=g, po=po: e.tensor_tensor(out=yoff, in0=po[:, :].rearrange("p (a c) -> p a c", a=8), in1=eA[:, g * 8:(g + 1) * 8, :].to_broadcast([128, 8, 64]), op=ALU.mult),
                     reads=[("ps", 6), "eA"], writes=["yoff"])
                S.op("dve", lambda e, g=g, py=py: e.tensor_tensor(out=y_[:, g * 512:(g + 1) * 512], in0=py[:, :], in1=yoff.rearrange("p a c -> p (a c)"), op=ALU.add),
                     reads=[pyk, "yoff"], writes=[("ych", yb_)])
            pst_ = PS[7]
            S.op("pe", lambda e, g=g, pst_=pst_: e.matmul(pst_[:, :], lhsT=xc[b][:, 2048 + g * 128:2048 + (g + 1) * 128], rhs=xdte[:, g * 8:(g + 1) * 8, :], start=True, stop=True),
                 reads=[("xc", b), "xdte"], writes=[("ps", 7)])
            S.op("dve", lambda e, g=g: e.tensor_tensor(out=hT[:, g, :].rearrange("p (a c) -> p a c", a=8), in0=hT[:, g, :].rearrange("p (a c) -> p a c", a=8), in1=dec[:, g * 8:(g + 1) * 8, :].to_broadcast([128, 8, 64]), op=ALU.mult),
                 reads=["hT", "dec"], writes=["hT"])
            S.op("dve", lambda e, g=g, pst_=pst_: e.tensor_tensor(out=hT[:, g, :], in0=pst_[:, :], in1=hT[:, g, :], op=ALU.add), reads=[("ps", 7), "hT"], writes=["hT"])
            S.op("pool", lambda e, g=g: e.tensor_copy(out=hb[:, g, :], in_=hT[:, g, :]), reads=["hT"], writes=["hb"])
        return b, yb_

    order_f = list(range(NTL, NT)) + list(range(NTL))
    order_b = list(range(NT - 1, NTL - 1, -1)) + list(range(NTL - 1, -1, -1))
    S.op("dve", lambda e: e.memset(hT, 0.0), writes=["hT"])
    S.op("pool", lambda e: e.memset(hb, 0.0), writes=["hb"])
    for tt in order_f:
        want = (tt < NTL) or ctx_out
        b, yb_ = chunk(tt, 0, want)
        if want:
            dma("sp", YF[tt * 128:(tt + 1) * 128, :], ych[yb_], reads=[("ych", yb_)], writes=[("YF", tt)], semkey=("ych", yb_))
    S.op("dve", lambda e: e.memset(hT, 0.0), reads=["hb"], writes=["hT"])
    S.op("pool", lambda e: e.memset(hb, 0.0), writes=["hb"])

    def finish(tt, b, yb_):
        y_ = ych[yb_]
        dma("sp", yfl, YF[tt * 128:(tt + 1) * 128, :], reads=[("YF", tt)], writes=["yfl"], semkey="yfl")
        dma("act", zt, ZT[tt * 128:(tt + 1) * 128, :], writes=["zt"], semkey="zt")
        xs3 = xc[b][:, 0:2048].rearrange("p (h c) -> p h c", h=32)
        S.op("pool", lambda e: e.tensor_tensor(out=y_, in0=y_, in1=yfl, op=ALU.add), reads=[("ych", yb_), "yfl"], writes=[("ych", yb_)])
        S.op("dve", lambda e: e.tensor_tensor(out=xdtf, in0=xs3, in1=dsk.to_broadcast([128, 32, 64]), op=ALU.mult), reads=[("xc", b), "dsk"], writes=["xdtf"])
        S.op("dve", lambda e: e.tensor_tensor(out=y_, in0=y_, in1=xdtf.rearrange("p a c -> p (a c)"), op=ALU.add), reads=[("ych", yb_), "xdtf"], writes=[("ych", yb_)])
        S.op("act", lambda e: e.activation(out=zs, in_=zt, func=AF.Silu), reads=["zt"], writes=["zs"])
        S.op("dve", lambda e: e.tensor_tensor(out=y_, in0=y_, in1=zs, op=ALU.mult), reads=[("ych", yb_), "zs"], writes=[("ych", yb_)])
        S.op("dve", lambda e: e.memset(ss, 0.0), writes=["ss"])
        for g in range(4):
            S.op("act", lambda e, g=g: e.activation(out=junk, in_=y_[:, g * 512:(g + 1) * 512], func=AF.Square, accum_out=ss[:, g:g + 1]), reads=[("ych", yb_), "ss"], writes=["ss", "junk"])
        S.op("dve", lambda e: e.tensor_scalar(out=ss, in0=ss, scalar1=1.0 / 512, scalar2=EPS, op0=ALU.mult, op1=ALU.add), reads=["ss"], writes=["ss"])
        S.op("act", lambda e: e.activation(out=ss, in_=ss, func=AF.Sqrt), reads=["ss"], writes=["ss"])
        S.op("dve", lambda e: e.reciprocal(out=ss, in_=ss), reads=["ss"], writes=["ss"])
        for g in range(4):
            S.op("dve", lambda e, g=g: e.scalar_tensor_tensor(out=vn[:, g * 512:(g + 1) * 512], in0=y_[:, g * 512:(g + 1) * 512], scalar=ss[:, g:g + 1], in1=gnorm[:, g * 512:(g + 1) * 512], op0=ALU.mult, op1=ALU.mult),
                 reads=[("ych", yb_), "ss", "gnorm"], writes=["vn"])
        for q in range(4):
            pb = q % 2
            pstb = PS[pb][:, :].bitcast(BF16)
            for j in range(4):
                kc = q * 4 + j
                S.op("pe", lambda e, j=j, kc=kc, pstb=pstb: e.transpose(out=pstb[:, j * 128:(j + 1) * 128], in_=vn[:, kc * 128:(kc + 1) * 128], identity=identb), reads=["vn"], writes=[("ps", pb)])
            S.op("act", lambda e, q=q, pstb=pstb: e.activation(out=ygs[q % 2], in_=pstb[:, 0:512].rearrange("p (a c) -> p a c", a=4), func=AF.Copy), reads=[("ps", pb)], writes=[("ygs", q % 2)])
            dma("sp", YG[2048 + q * 512:2048 + (q + 1) * 512, tt * 128:(tt + 1) * 128].rearrange("(j p) t -> p j t", p=128), ygs[q % 2], reads=[("ygs", q % 2)], writes=[("YGc", q, tt)], semkey=("ygs", q % 2))

    for tt in order_b:
        want = (tt < NTL) or ctx_out
        b, yb_ = chunk(tt, 1, want)
        if want:
            finish(tt, b, yb_)


def phase_H(S, AR, A_KEEP, PS, l, W, PJ, YG, xsrc, xdst, gbc, blocks, T, NT, NTL, SEQ, ctx_out, last, dma, WB16, WO16):
    S.barrier()
    AR.off = A_KEEP
    lng = AR.alloc([2048], F32)
    lnb = AR.alloc([2048], F32)
    dma("sp", lng, W["ln_g"][l:l + 1, :].partition_broadcast(128), writes=["lng"], semkey="h0")
    dma("sp", lnb, W["ln_b"][l:l + 1, :].partition_broadcast(128), writes=["lnb"], semkey="h1")
    NBM = 512
    ygb = AR.alloc([32, NBM], BF16)
    wbr = [AR.alloc([32, 128], BF16) for _ in range(2)]
    mT = AR.alloc([16, NBM], BF16)
    wo = [AR.alloc([16, 512], BF16) for _ in range(2)]
    xt = [AR.alloc([2048], F32) for _ in range(4)]
    mg = [AR.alloc([3, NBM], BF16) for _ in range(2)]
    sg = [AR.alloc([3, NBM], F32) for _ in range(2)]
    ta = [AR.alloc([3, NBM], F32) for _ in range(2)]
    tg = [AR.alloc([512], F32) for _ in range(2)]
    stats = AR.alloc([4, 6], F32)
    mv = AR.alloc([2], F32)
    rstd = AR.alloc([1], F32)
    wbrv = W["w_br"][l]
    wov = W["w_out"][l]
    mgv = PJ[O_MG:O_MG + 6144, :].rearrange("(br f p) t -> p br f t", br=3, p=128)
    cnt = {"w": 0, "o": 0, "g": 0}

    def do_block(t0, NB, first):
        r = 0 if t0 < SEQ else 1
        ntt = NB // 128
        for k0_ in range(0, 32, 8):
            dma("sp", ygb[:, k0_:k0_ + 8, 0:NB], YG[k0_ * 128:(k0_ + 8) * 128, t0:t0 + NB].rearrange("(k p) t -> p k t", p=128), writes=["ygb"], semkey="ygb")
        for tt in range(ntt):
            dma("act", xt[tt], xsrc[t0 + tt * 128:t0 + (tt + 1) * 128, :], writes=[("xt", tt)], semkey=("xt", tt))
        for f in range(16):
            wi = cnt["w"] % 2
            cnt["w"] += 1
            wk = ("wbr", wi)
            if first:
                dma("pool", wbr[wi], wbrv[:, f * 128:(f + 1) * 128].rearrange("(k p) c -> p k c", p=128), writes=[wk], semkey=wk)
                dma("act", WB16[f], wbr[wi].rearrange("p k c -> p (k c)"), reads=[wk], writes=[("WB16", f)], semkey=("wbs", wi))
            else:
                dma("sp", wbr[wi].rearrange("p k c -> p (k c)"), WB16[f], reads=[("WB16", f)], writes=[wk], semkey=wk)
            m = f % 2
            dma("sp", mg[m][:, :, 0:NB], mgv[:, :, f, t0:t0 + NB], writes=[("mg", m)], semkey=("mg", m))
            S.op("act", lambda e, m=m: e.activation(out=sg[m][:, :, 0:NB], in_=mg[m][:, :, 0:NB], func=AF.Sigmoid), reads=[("mg", m)], writes=[("sg", m)])
            for br, (k0, k1) in enumerate(((0, 8), (8, 16), (16, 32))):
                pb = (f % 2) * 3 + br
                for k in range(k0, k1):
                    S.op("pe", lambda e, wi=wi, k=k, pb=pb, k0=k0, k1=k1: e.matmul(PS[pb][:, 0:NB], lhsT=wbr[wi][:, k, :], rhs=ygb[:, k, 0:NB], start=(k == k0), stop=(k == k1 - 1)),
                         reads=[wk, "ygb"], writes=[("ps", pb)])
                S.op("dve", lambda e, m=m, br=br, pb=pb: e.tensor_tensor(out=ta[m][:, br, 0:NB], in0=PS[pb][:, 0:NB], in1=sg[m][:, br, 0:NB], op=ALU.mult),
                     reads=[("ps", pb), ("sg", m)], writes=[("ta", m)])
            S.op("pool", lambda e, m=m: e.tensor_tensor(out=ta[m][:, 0, 0:NB], in0=ta[m][:, 0, 0:NB], in1=ta[m][:, 1, 0:NB], op=ALU.add), reads=[("ta", m)], writes=[("ta", m)])
            S.op("pool", lambda e, m=m, f=f: e.tensor_tensor(out=mT[:, f, 0:NB], in0=ta[m][:, 0, 0:NB], in1=ta[m][:, 2, 0:NB], op=ALU.add), reads=[("ta", m)], writes=["mT"])
        for cb in range(4):
            oi = cnt["o"] % 2
            cnt["o"] += 1
            ok_ = ("wo", oi)
            if first:
                dma("pool", wo[oi], wov[:, cb * 512:(cb + 1) * 512].rearrange("(f p) c -> p f c", p=128), writes=[ok_], semkey=ok_)
                dma("act", WO16[cb], wo[oi].rearrange("p f c -> p (f c)"), reads=[ok_], writes=[("WO16", cb)], semkey=("wos", oi))
            else:
                dma("sp", wo[oi].rearrange("p f c -> p (f c)"), WO16[cb], reads=[("WO16", cb)], writes=[ok_], semkey=ok_)
            for tt in range(ntt):
                gi = cnt["g"] % 2
                cnt["g"] += 1
                pb = 6 + gi
                for f in range(16):
                    S.op("pe", lambda e, oi=oi, f=f, tt=tt, pb=pb: e.matmul(PS[pb][:, :], lhsT=mT[:, f, tt * 128:(tt + 1) * 128], rhs=wo[oi][:, f, :], start=(f == 0), stop=(f == 15)),
                         reads=[ok_, "mT"], writes=[("ps", pb)])
                S.op("dve", lambda e, pb=pb, cb=cb, gi=gi: e.tensor_tensor(out=tg[gi], in0=PS[pb][:, :], in1=gbc[:, r, cb * 512:(cb + 1) * 512], op=ALU.mult), reads=[("ps", pb)], writes=[("tg", gi)])
                S.op("dve", lambda e, tt=tt, cb=cb, gi=gi: e.scalar_tensor_tensor(out=xt[tt][:, cb * 512:(cb + 1) * 512], in0=xt[tt][:, cb * 512:(cb + 1) * 512], scalar=ALPHA, in1=tg[gi], op0=ALU.mult, op1=ALU.add),
                     reads=[("xt", tt), ("tg", gi)], writes=[("xt", tt)])
        for tt in range(ntt):
            xk = ("xt", tt)
            for c in range(4):
                S.op("dve", lambda e, tt=tt, c=c: e.bn_stats(out=stats[:, c, :], in_=xt[tt][:, c * 512:(c + 1) * 512]), reads=[xk], writes=["st"])
            S.op("dve", lambda e: e.bn_aggr(out=mv, in_=stats), reads=["st"], writes=["mv"])
            S.op("dve", lambda e: e.tensor_scalar_add(out=rstd, in0=mv[:, 1:2], scalar1=EPS), reads=["mv"], writes=["rs"])
            S.op("act", lambda e: e.activation(out=rstd, in_=rstd, func=AF.Sqrt), reads=["rs"], writes=["rs"])
            S.op("dve", lambda e: e.reciprocal(out=rstd, in_=rstd), reads=["rs"], writes=["rs"])
            S.op("dve", lambda e, tt=tt: e.tensor_scalar(out=xt[tt], in0=xt[tt], scalar1=mv[:, 0:1], scalar2=rstd[:, 0:1], op0=ALU.subtract, op1=ALU.mult), reads=[xk, "mv", "rs"], writes=[xk])
            S.op("pool", lambda e, tt=tt: e.tensor_tensor(out=xt[tt], in0=xt[tt], in1=lng, op=ALU.mult), reads=[xk, "lng"], writes=[xk])
            S.op("pool", lambda e, tt=tt: e.tensor_tensor(out=xt[tt], in0=xt[tt], in1=lnb, op=ALU.add), reads=[xk, "lnb"], writes=[xk])
            dma("sp", xdst[t0 + tt * 128:t0 + (tt + 1) * 128, :], xt[tt], reads=[xk], writes=[("xo", t0, tt)], semkey=("xts", tt))

    for bi_, (t0, n) in enumerate(blocks):
        if t0 >= SEQ and not ctx_out:
            continue
        do_block(t0, n, bi_ == 0)


_CACHE = {}


def _prep_inputs(inp, b, seq, ctx):
    m = {
        "xin": np.ascontiguousarray(np.concatenate([inp["x"][b], inp["ctx"][b]], 0)),
        "c2": np.ascontiguousarray(np.stack([inp["c"][b], inp["c_ctx"]], 0)),
        "ssd_a_log": np.ascontiguousarray(inp["ssd_a_log"].reshape(inp["ssd_a_log"].shape[0], 64)),
        "ssd_dt_bias": np.ascontiguousarray(inp["ssd_dt_bias"].reshape(inp["ssd_dt_bias"].shape[0], 64)),
        "w_br": np.ascontiguousarray(np.concatenate([inp["w_br_a"], inp["w_br_b"], inp["w_br_c"]], 1)),
    }
    for k in ("w_mod", "b_mod", "w_in", "mla_q_norm", "mla_w_uq", "mla_kv_norm", "mla_w_ukv", "gqa_q_norm", "gqa_k_norm",
              "ssd_conv_w", "ssd_conv_b", "ssd_d", "ssd_norm", "w_out", "ln_g", "ln_b"):
        m[k] = np.ascontiguousarray(inp[k])
    return m


def kernel(**inputs):
    inp = {k: np.asarray(v, dtype=np.float32) for k, v in inputs.items()}
    B, SEQ, _ = inp["x"].shape
    CTX = inp["ctx"].shape[1]
    depth = inp["w_in"].shape[0]
    keyc = (SEQ, CTX, depth)
    if keyc not in _CACHE:
        _CACHE[keyc] = build_program(SEQ, CTX, depth)
    nc = _CACHE[keyc]
    consts = host_consts(SEQ, CTX)
    shared = _prep_inputs(inp, 0, SEQ, CTX)
    in_maps = []
    ncores = 8
    for core in range(ncores):
        b = core % B
        m = dict(shared)
        m["xin"] = np.ascontiguousarray(np.concatenate([inp["x"][b], inp["ctx"][b]], 0))
        m["c2"] = np.ascontiguousarray(np.stack([inp["c"][b], inp["c_ctx"]], 0))
        m.update(consts)
        in_maps.append(m)
    res = run_bass_kernel_spmd(nc, in_maps, core_ids=list(range(ncores)))
    return np.stack([np.asarray(res.results[b]["out"], dtype=np.float32) for b in range(B)], 0)
```
